# Optimizing a Trainium2 kernel written in Bass

```python
import jax, jax.numpy as jnp
from jax import lax
import numpy as np

D_MODEL = 2048
BATCH = 2
SEQ = 16384
DEPTH = 4
DEC_BATCH = 4
DEC_SEQ = 4096
PAST_LEN = 128

N_MIXERS = 4
EPS = 1e-6
D_FF = ((8 * D_MODEL // 3 + 255) // 256) * 256
POOL_WINDOWS = (2, 4, 8, 16)
N_POOL_GROUPS = len(POOL_WINDOWS)
POOL_GROUP_DIM = D_MODEL // N_POOL_GROUPS
SC_WIDTH = 3
CF_WIDTH = 31
SG_CHUNK = 128
SG_HEADS = 8
SG_HEAD_DIM = D_MODEL // SG_HEADS
N_LAYERS_A = (DEPTH + 3) // 4
N_LAYERS_B = (DEPTH + 2) // 4
N_LAYERS_C = (DEPTH + 1) // 4
N_LAYERS_D = DEPTH // 4

kernel_name = "hybrid_pool_conv_conformer_sgu_encoder"


def rms_norm(x, g):
    xf = x.astype(jnp.float32)
    y = xf * lax.rsqrt(jnp.mean(xf * xf, axis=-1, keepdims=True) + EPS)
    return (y * g.astype(jnp.float32)).astype(x.dtype)


def layer_norm(x, g, b):
    xf = x.astype(jnp.float32)
    mu = jnp.mean(xf, axis=-1, keepdims=True)
    xc = xf - mu
    y = xc * lax.rsqrt(jnp.mean(xc * xc, axis=-1, keepdims=True) + EPS)
    return (y * g.astype(jnp.float32) + b.astype(jnp.float32)).astype(x.dtype)


def depthwise_conv(x, w):
    k = w.shape[0]
    return lax.conv_general_dilated(
        x, w[:, None, :], window_strides=(1,), padding=((k // 2, k // 2),),
        dimension_numbers=("NWC", "WIO", "NWC"), feature_group_count=x.shape[-1])


def pool_mixer(x, w_in, w_grp, scale, w_out):
    b, s, _ = x.shape
    h = (x @ w_in).reshape(b, s, N_POOL_GROUPS, POOL_GROUP_DIM)
    csum = jnp.cumsum(h.astype(jnp.float32), axis=1)
    csum = jnp.concatenate([jnp.zeros_like(csum[:, :1]), csum], axis=1)
    t = jnp.arange(s, dtype=jnp.int32)
    pooled = []
    for g, w in enumerate(POOL_WINDOWS):
        half = w // 2
        cg = jnp.pad(csum[:, :, g], ((0, 0), (half, half), (0, 0)), mode="edge")
        total = cg[:, w:w + s] - cg[:, :s]
        count = (jnp.minimum(t + half, s) - jnp.maximum(t - half, 0)).astype(jnp.float32)
        pooled.append(total / count[None, :, None])
    pooled = jnp.stack(pooled, axis=2).astype(x.dtype) - h
    mixed = jnp.einsum("bsgc,gcd->bsgd", pooled, w_grp).reshape(b, s, D_MODEL) * scale
    return mixed @ w_out


def short_conv_mixer(x, w_in, conv_w, w_out):
    gb, gc, h = jnp.split(x @ w_in, 3, axis=-1)
    return (gb * depthwise_conv(gc * h, conv_w)) @ w_out


def conformer_mixer(x, w_in, dw_w, dw_b, ln_g, ln_b, w_out):
    a, gate = jnp.split(x @ w_in, 2, axis=-1)
    h = a * jax.nn.sigmoid(gate)
    h = depthwise_conv(h, dw_w) + dw_b
    h = jax.nn.silu(layer_norm(h, ln_g, ln_b))
    return h @ w_out


def spatial_gating_mixer(x, w_in, ln_g, ln_b, w_s, b_s, w_out):
    b, s, _ = x.shape
    u, v = jnp.split(jax.nn.gelu(x @ w_in, approximate=False), 2, axis=-1)
    v = layer_norm(v, ln_g, ln_b).reshape(b, s // SG_CHUNK, SG_CHUNK, SG_HEADS, SG_HEAD_DIM)
    v = jnp.einsum("bnphc,hqp->bnqhc", v, w_s) + jnp.transpose(b_s)[:, :, None]
    return (u * v.reshape(b, s, D_MODEL)) @ w_out


def swiglu(x, w_gate, w_up, w_down):
    return (jax.nn.silu(x @ w_gate) * (x @ w_up)) @ w_down


def run_trunk(x, mix_pre_g, mix_post_g, ffn_pre_g, ffn_post_g, ffn_w_gate, ffn_w_up,
              ffn_w_down, pool_w_in, pool_w_grp, pool_scale, pool_w_out, sc_w_in,
              sc_conv_w, sc_w_out, cf_w_in, cf_dw_w, cf_dw_b, cf_ln_g, cf_ln_b,
              cf_w_out, sg_w_in, sg_ln_g, sg_ln_b, sg_w_s, sg_b_s, sg_w_out):
    for i in range(DEPTH):
        kind, j = i % N_MIXERS, i // N_MIXERS
        h = rms_norm(x, mix_pre_g[i])
        if kind == 0:
            h = pool_mixer(h, pool_w_in[j], pool_w_grp[j], pool_scale[j], pool_w_out[j])
        elif kind == 1:
            h = short_conv_mixer(h, sc_w_in[j], sc_conv_w[j], sc_w_out[j])
        elif kind == 2:
            h = conformer_mixer(h, cf_w_in[j], cf_dw_w[j], cf_dw_b[j], cf_ln_g[j],
                                cf_ln_b[j], cf_w_out[j])
        else:
            h = spatial_gating_mixer(h, sg_w_in[j], sg_ln_g[j], sg_ln_b[j], sg_w_s[j],
                                     sg_b_s[j], sg_w_out[j])
        x = x + rms_norm(h, mix_post_g[i])
        h = swiglu(rms_norm(x, ffn_pre_g[i]), ffn_w_gate[i], ffn_w_up[i], ffn_w_down[i])
        x = x + rms_norm(h, ffn_post_g[i])
    return x


def setup_inputs(seed: int = 0) -> dict:
    key = jax.random.key(seed)
    k = jax.random.split(key, 32)
    f32 = jnp.float32

    def dense(kk, shape, fan_in):
        return jax.random.normal(kk, shape, f32) * (fan_in ** -0.5)

    def gain(kk, shape):
        return 1.0 + 0.05 * jax.random.normal(kk, shape, f32)

    def bias(kk, shape):
        return 0.02 * jax.random.normal(kk, shape, f32)

    D = D_MODEL
    return {
        "x_prompt": jax.random.normal(k[0], (BATCH, SEQ, D), f32),
        "x_sample": jax.random.normal(k[1], (DEC_BATCH, DEC_SEQ, D), f32),
        "mix_pre_g": gain(k[2], (DEPTH, D)),
        "mix_post_g": gain(k[3], (DEPTH, D)),
        "ffn_pre_g": gain(k[4], (DEPTH, D)),
        "ffn_post_g": gain(k[5], (DEPTH, D)),
        "ffn_w_gate": dense(k[6], (DEPTH, D, D_FF), D),
        "ffn_w_up": dense(k[7], (DEPTH, D, D_FF), D),
        "ffn_w_down": dense(k[8], (DEPTH, D_FF, D), D_FF),
        "pool_w_in": dense(k[9], (N_LAYERS_A, D, D), D),
        "pool_w_grp": dense(k[10], (N_LAYERS_A, N_POOL_GROUPS, POOL_GROUP_DIM, POOL_GROUP_DIM), POOL_GROUP_DIM),
        "pool_scale": gain(k[11], (N_LAYERS_A, D)),
        "pool_w_out": dense(k[12], (N_LAYERS_A, D, D), D),
        "sc_w_in": dense(k[13], (N_LAYERS_B, D, 3 * D), D),
        "sc_conv_w": dense(k[14], (N_LAYERS_B, SC_WIDTH, D), SC_WIDTH),
        "sc_w_out": dense(k[15], (N_LAYERS_B, D, D), D),
        "cf_w_in": dense(k[16], (N_LAYERS_C, D, 2 * D), D),
        "cf_dw_w": dense(k[17], (N_LAYERS_C, CF_WIDTH, D), CF_WIDTH),
        "cf_dw_b": bias(k[18], (N_LAYERS_C, D)),
        "cf_ln_g": gain(k[19], (N_LAYERS_C, D)),
        "cf_ln_b": bias(k[20], (N_LAYERS_C, D)),
        "cf_w_out": dense(k[21], (N_LAYERS_C, D, D), D),
        "sg_w_in": dense(k[22], (N_LAYERS_D, D, 2 * D), D),
        "sg_ln_g": gain(k[23], (N_LAYERS_D, D)),
        "sg_ln_b": bias(k[24], (N_LAYERS_D, D)),
        "sg_w_s": dense(k[25], (N_LAYERS_D, SG_HEADS, SG_CHUNK, SG_CHUNK), SG_CHUNK),
        "sg_b_s": gain(k[26], (N_LAYERS_D, SG_HEADS, SG_CHUNK)),
        "sg_w_out": dense(k[27], (N_LAYERS_D, D, D), D),
    }


def reference(x_prompt, x_sample, mix_pre_g, mix_post_g, ffn_pre_g, ffn_post_g,
              ffn_w_gate, ffn_w_up, ffn_w_down, pool_w_in, pool_w_grp, pool_scale,
              pool_w_out, sc_w_in, sc_conv_w, sc_w_out, cf_w_in, cf_dw_w, cf_dw_b,
              cf_ln_g, cf_ln_b, cf_w_out, sg_w_in, sg_ln_g, sg_ln_b, sg_w_s, sg_b_s,
              sg_w_out):
    y_prompt = run_trunk(x_prompt, mix_pre_g, mix_post_g, ffn_pre_g, ffn_post_g,
                         ffn_w_gate, ffn_w_up, ffn_w_down, pool_w_in, pool_w_grp,
                         pool_scale, pool_w_out, sc_w_in, sc_conv_w, sc_w_out, cf_w_in,
                         cf_dw_w, cf_dw_b, cf_ln_g, cf_ln_b, cf_w_out, sg_w_in, sg_ln_g,
                         sg_ln_b, sg_w_s, sg_b_s, sg_w_out)
    y_sample = run_trunk(x_sample, mix_pre_g, mix_post_g, ffn_pre_g, ffn_post_g,
                         ffn_w_gate, ffn_w_up, ffn_w_down, pool_w_in, pool_w_grp,
                         pool_scale, pool_w_out, sc_w_in, sc_conv_w, sc_w_out, cf_w_in,
                         cf_dw_w, cf_dw_b, cf_ln_g, cf_ln_b, cf_w_out, sg_w_in, sg_ln_g,
                         sg_ln_b, sg_w_s, sg_b_s, sg_w_out)
    return (y_prompt, y_sample)
```

```python
import numpy as np
import concourse.bass as bass
import concourse.mybir as mybir
from concourse.bass_utils import run_bass_kernel_spmd

F32 = mybir.dt.float32
BF16 = mybir.dt.bfloat16
AF = mybir.ActivationFunctionType
ALU = mybir.AluOpType
AX = mybir.AxisListType

D = 2048
KC = 16
DFF = 5632
FC = 44
HL = 24
CORE = 512
TC = CORE + 2 * HL
PADW = 8
HTW = TC + 2 * PADW
EPS = 1e-6
NSLOT = 3
SLOT_ELEMS = 4096
NV = 56
POOL_WINDOWS = (2, 4, 8, 16)
SCRW = 592

V_MIX_PRE, V_MIX_POST, V_FFN_PRE, V_FFN_POST = 0, 4, 8, 12
V_POOL_SCALE, V_CF_DWB, V_CF_LNG, V_CF_LNB, V_SG_LNG, V_SG_LNB = 16, 17, 18, 19, 20, 21
V_SC_CONV, V_CF_DW = 22, 25


class Sched:
    def __init__(self, engs, semh):
        self.E = engs
        self.semh = semh
        self.cnt = {k: 0 for k in semh}
        self.known = {e: {} for e in engs}
        self.cells = {}
        self.cellsz = {}
        self.dry = False
        self.prog = {e: [] for e in engs}

    def alloc(self, name, cell_bytes):
        self.cellsz[name] = cell_bytes

    def _cells(self, regs):
        for (a, b0, b1) in regs:
            cs = self.cellsz[a]
            for c in range(b0 // cs, (b1 - 1) // cs + 1):
                yield (a, c)

    def _gather(self, reads, writes):
        deps = {}

        def add(k, v):
            if deps.get(k, 0) < v:
                deps[k] = v
        for c in self._cells(reads):
            st = self.cells.get(c)
            if st and st[0] is not None:
                add(*st[0])
        for c in self._cells(writes):
            st = self.cells.get(c)
            if st:
                if st[0] is not None:
                    add(*st[0])
                for k, v in st[1].items():
                    add(k, v)
        return deps

    def _record(self, tok, reads, writes):
        k, v = tok
        for c in self._cells(writes):
            self.cells[c] = [tok, {}]
        for c in self._cells(reads):
            st = self.cells.setdefault(c, [None, {}])
            if st[1].get(k, 0) < v:
                st[1][k] = v

    def _waits(self, eng, deps):
        kn = self.known[eng]
        for k, v in deps.items():
            if k == eng and eng == 'pe':
                continue
            if kn.get(k, 0) >= v:
                continue
            self.prog[eng].append(lambda e, s=self.semh[k], v=v: e.wait_ge(s, v))
            kn[k] = v

    def op(self, eng, reads, writes, emit):
        if self.dry:
            return
        self._waits(eng, self._gather(reads, writes))
        self.cnt[eng] += 1
        self.prog[eng].append(lambda e, emit=emit, s=self.semh[eng]: emit(e).then_inc(s, 1))
        self._record((eng, self.cnt[eng]), reads, writes)

    def dma(self, eng, semname, reads, writes, emit):
        if self.dry:
            return
        self._waits(eng, self._gather(reads, writes))
        self.cnt[semname] += 16
        self.prog[eng].append(lambda e, emit=emit, s=self.semh[semname]: emit(e).then_inc(s, 16))
        self._record((semname, self.cnt[semname]), reads, writes)

    def final_wait(self, eng, keys):
        for k in keys:
            if self.cnt[k] > 0:
                self.prog[eng].append(lambda e, s=self.semh[k], v=self.cnt[k]: e.wait_ge(s, v))


def build_program(NT):
    nc = bass.Bass("TRN2", target_bir_lowering=False)

    def din(name, shape):
        return nc.dram_tensor(name, list(shape), F32, kind="ExternalInput").ap()

    xin = din("xin", [NT, D, TC])
    maskd = din("maskb", [NT, 128, TC])
    vecsd = din("vecs", [128, NV * KC])
    wsTd = din("wsT", [128, 8 * 128])
    bsbd = din("bsb", [128, 8 * 128])
    identd = din("ident", [128, 128])
    dgw = din("dgw", [KC, 128, 31 * 128])
    w_gate = din("ffn_w_gate", [4, D, DFF])
    w_up = din("ffn_w_up", [4, D, DFF])
    w_down = din("ffn_w_down", [4, DFF, D])
    pool_w_in = din("pool_w_in", [D, D])
    pool_w_grp = din("pool_w_grp", [4, 512, 512])
    pool_w_out = din("pool_w_out", [D, D])
    sc_w_in = din("sc_w_in", [D, 3 * D])
    sc_w_out = din("sc_w_out", [D, D])
    cf_w_in = din("cf_w_in", [D, 2 * D])
    cf_w_out = din("cf_w_out", [D, D])
    sg_w_in = din("sg_w_in", [D, 2 * D])
    sg_w_out = din("sg_w_out", [D, D])
    yout = nc.dram_tensor("yout", [NT, D, CORE], F32, kind="ExternalOutput").ap()

    from contextlib import ExitStack
    with ExitStack() as es:
        def sb(name, shape, dt):
            return es.enter_context(nc.sbuf_tensor(name, list(shape), dt))

        x_t = sb("x_t", [128, KC, TC], F32)
        xn_t = sb("xn_t", [128, KC * TC], BF16)
        ht_t = sb("ht_t", [128, KC, HTW], F32)
        U_t = sb("U_t", [128, FC * TC], BF16)
        ring_t = sb("ring_t", [128, NSLOT, SLOT_ELEMS], BF16)
        vecs_t = sb("vecs_t", [128, NV, KC], F32)
        maskp_t = sb("maskp_t", [128, HTW], F32)
        sgt_t = sb("sgt_t", [128, 2, 2, TC], F32)
        scr_t = sb("scr_t", [128, 4, SCRW], F32)
        st_t = sb("st_t", [128, 4, TC], F32)
        wsT_t = sb("wsT_t", [128, 8, 128], BF16)
        bsb_t = sb("bsb_t", [128, 8, 128], F32)
        R_t = sb("R_t", [128, 8, 128], F32)
        ones_f = sb("ones_f", [128, 128], F32)
        ones_b = sb("ones_b", [128, 128], BF16)
        ident_b = sb("ident_b", [128, 128], BF16)
        tiny_t = sb("tiny_t", [128, 64], F32)
        psb = [es.enter_context(nc.psum_tensor(f"ps{b}", [128, 512], F32)) for b in range(8)]

        semnames = ['pe', 'act', 'dve', 'pool', 'sp', 'mld', 'yst', 'cst'] + [f'xl{k}' for k in range(KC)] + [f'slot{s}' for s in range(NSLOT)]
        semh = {k: es.enter_context(nc.semaphore(k)) for k in semnames}
        engs = {k: None for k in ('pe', 'act', 'dve', 'pool', 'sp')}
        S = Sched(engs, semh)
        S.alloc('x', 1120)
        S.alloc('xn', 560)
        S.alloc('ht', HTW * 4)
        S.alloc('U', 560)
        S.alloc('ring', SLOT_ELEMS * 2)
        S.alloc('vecs', 1 << 20)
        S.alloc('mask', 1 << 20)
        S.alloc('sgt', 2 * TC * 4)
        S.alloc('scr', SCRW * 4)
        S.alloc('st', TC * 4)
        S.alloc('const', 1 << 20)
        S.alloc('R', 1 << 20)
        S.alloc('tiny', 4)
        S.alloc('ps', 2048)

        def r_x(k, c0, n): return ('x', (k * TC + c0) * 4, (k * TC + c0 + n) * 4)
        def r_xn(k, c0, n): return ('xn', (k * TC + c0) * 2, (k * TC + c0 + n) * 2)
        def r_ht(k, e0, n): return ('ht', (k * HTW + e0) * 4, (k * HTW + e0 + n) * 4)
        def r_U(b0, b1): return ('U', b0, b1)
        def r_ps(b): return ('ps', b * 2048, (b + 1) * 2048)
        def r_scr(r, c0=0, n=SCRW): return ('scr', (r * SCRW + c0) * 4, (r * SCRW + c0 + n) * 4)
        def r_st(r): return ('st', r * TC * 4, (r + 1) * TC * 4)
        def r_sgt(b): return ('sgt', b * 2 * TC * 4, (b + 1) * 2 * TC * 4)
        def r_tiny(c, n=1): return ('tiny', c * 4, (c + n) * 4)
        R_VECS = ('vecs', 0, 1)
        R_MASK = ('mask', 0, 1)
        R_CONST = ('const', 0, 1)
        R_R = ('R', 0, 1)

        def U_bf(k, c0, n):
            return U_t[:, k * TC + c0: k * TC + c0 + n], r_U((k * TC + c0) * 2, (k * TC + c0 + n) * 2)

        def U_f32(k, c0, n, base=0):
            b0 = base + (k * TC + c0) * 4
            return (U_t[:, b0 // 2: b0 // 2 + 2 * n].bitcast(F32), r_U(b0, b0 + 4 * n))

        def vec(v, k):
            return vecs_t[:, v, k:k + 1]

        plan = []
        state = {'next_consume': 0, 'next_issue': 0, 'half': 0}

        def issue_slab(i):
            W2d, k0, nk, c0, ncol = plan[i]
            s = i % NSLOT
            src = W2d[k0 * 128:(k0 + nk) * 128, c0:c0 + ncol].rearrange("(k p) c -> p k c", p=128)
            dst = ring_t[:, s, 0:nk * ncol].rearrange("p (k c) -> p k c", c=ncol)
            S.dma('pool', f'slot{s}', [], [('ring', s * SLOT_ELEMS * 2, (s + 1) * SLOT_ELEMS * 2)],
                  lambda e: e.dma_start(out=dst, in_=src))

        def next_slab(W2d, k0, nk, c0, ncol):
            assert nk * ncol <= SLOT_ELEMS
            if S.dry:
                plan.append((W2d, k0, nk, c0, ncol))
                return None, None, None
            i = state['next_consume']
            assert plan[i][1:] == (k0, nk, c0, ncol), (i, plan[i][1:], (k0, nk, c0, ncol))
            state['next_consume'] += 1
            while state['next_issue'] < min(len(plan), i + 1):
                issue_slab(state['next_issue'])
                state['next_issue'] += 1
            s = i % NSLOT
            view = ring_t[:, s, 0:nk * ncol].rearrange("p (k c) -> p k c", c=ncol)
            return view, ('ring', s * SLOT_ELEMS * 2, (s + 1) * SLOT_ELEMS * 2), i

        def after_consume(i):
            if S.dry:
                return
            while state['next_issue'] < min(len(plan), i + NSLOT + 1):
                j = state['next_issue']
                if j - NSLOT > i:
                    break
                issue_slab(j)
                state['next_issue'] += 1

        def take_half(nb):
            h = state['half']
            state['half'] ^= 1
            return [h * 4 + b for b in range(nb)]

        def proj_fm(W2d, nkc, kgroups, colbase, n_mpairs, act_fn, pieces, evac_fn, before_evac=None):
            kper = nkc // kgroups
            for mp in range(n_mpairs):
                banks = take_half(2 * len(pieces))
                for kg in range(kgroups):
                    slab, sreg, si = next_slab(W2d, kg * kper, kper, colbase + mp * 256, 256)
                    if S.dry:
                        continue
                    writes = [r_ps(b) for b in banks]
                    if before_evac is not None and mp == 0:
                        for kk in range(kper):
                            def emit_k(e, slab=slab, kg=kg, banks=banks, kk=kk):
                                inst = None
                                k = kg * kper + kk
                                for m in range(2):
                                    for pi, (c0, n) in enumerate(pieces):
                                        b = banks[m * len(pieces) + pi]
                                        inst = e.matmul(psb[b][:, 0:n], lhsT=slab[:, kk, m * 128:(m + 1) * 128],
                                                        rhs=act_fn(k, c0, n)[0],
                                                        start=(k == 0), stop=(k == nkc - 1))
                                return inst
                            S.op('pe', [sreg] + [act_fn(kg * kper + kk, c0, n)[1] for (c0, n) in pieces], writes, emit_k)
                        after_consume(si)
                        continue
                    reads = [sreg]
                    for kk in range(kper):
                        for (c0, n) in pieces:
                            reads.append(act_fn(kg * kper + kk, c0, n)[1])

                    def emit(e, slab=slab, kg=kg, banks=banks):
                        inst = None
                        for m in range(2):
                            for kk in range(kper):
                                k = kg * kper + kk
                                for pi, (c0, n) in enumerate(pieces):
                                    b = banks[m * len(pieces) + pi]
                                    inst = e.matmul(psb[b][:, 0:n], lhsT=slab[:, kk, m * 128:(m + 1) * 128],
                                                    rhs=act_fn(k, c0, n)[0],
                                                    start=(k == 0), stop=(k == nkc - 1))
                        return inst
                    S.op('pe', reads, writes, emit)
                    after_consume(si)
                if S.dry:
                    continue
                if before_evac is not None and mp == 0:
                    before_evac()
                for m in range(2):
                    for pi, (c0, n) in enumerate(pieces):
                        b = banks[m * len(pieces) + pi]
                        evac_fn(mp * 2 + m, pi, c0, n, psb[b][:, 0:n], r_ps(b))

        def sq_view(where, k, c0, n):
            if where == 'U':
                return U_bf(k, c0, n)
            if where == 'U2':
                return U_bf(28 + k, c0, n)
            return xn_t[:, k * TC + c0: k * TC + c0 + n], r_xn(k, c0, n)

        def stats_rstd(where, pieces, cols, out_row, epsv_row=None, keep_sqrt_row=None):
            c_lo, c_n = cols
            banks = take_half(len(pieces))
            for k in range(KC):
                def emit_k(e, k=k):
                    inst = None
                    for pi, (c0, n) in enumerate(pieces):
                        inst = e.matmul(psb[banks[pi]][:, 0:n], lhsT=ones_b[:, :], rhs=sq_view(where, k, c0, n)[0],
                                        start=(k == 0), stop=(k == KC - 1))
                    return inst
                S.op('pe', [R_CONST] + [sq_view(where, k, c0, n)[1] for (c0, n) in pieces], [r_ps(b) for b in banks], emit_k)
            srow = out_row if keep_sqrt_row is None else keep_sqrt_row
            for pi, (c0, n) in enumerate(pieces):
                if epsv_row is None:
                    S.op('act', [r_ps(banks[pi]), r_tiny(0)], [r_st(srow)],
                         lambda e, pi=pi, c0=c0, n=n: e.activation(out=st_t[:, srow, c0:c0 + n], in_=psb[banks[pi]][:, 0:n],
                                                                    func=AF.Sqrt, bias=tiny_t[:, 0:1], scale=1.0 / D))
                else:
                    S.op('dve', [r_ps(banks[pi]), r_st(epsv_row)], [r_st(srow)],
                         lambda e, pi=pi, c0=c0, n=n: e.scalar_tensor_tensor(
                             out=st_t[:, srow, c0:c0 + n], in0=psb[banks[pi]][:, 0:n], scalar=1.0 / D,
                             in1=st_t[:, epsv_row, c0:c0 + n], op0=ALU.mult, op1=ALU.add))
                    S.op('act', [r_st(srow)], [r_st(srow)],
                         lambda e, c0=c0, n=n: e.activation(out=st_t[:, srow, c0:c0 + n], in_=st_t[:, srow, c0:c0 + n],
                                                            func=AF.Sqrt))
            S.op('dve', [r_st(srow)], [r_st(out_row)],
                 lambda e: e.reciprocal(out=st_t[:, out_row, c_lo:c_lo + c_n], in_=st_t[:, srow, c_lo:c_lo + c_n]))

        def prenorm(gv, cols, pieces):
            c_lo, c_n = cols
            for k in range(KC):
                o, ro = sq_view('U', k, c_lo, c_n)
                S.op('act', [r_x(k, c_lo, c_n)], [ro],
                     lambda e, k=k, o=o: e.activation(out=o, in_=x_t[:, k, c_lo:c_lo + c_n], func=AF.Square))
            stats_rstd('U', pieces, cols, 0)
            for k in range(KC):
                eng = 'dve'
                S.op(eng, [r_x(k, c_lo, c_n), r_st(0), R_VECS], [r_xn(k, c_lo, c_n)],
                     lambda e, k=k: e.scalar_tensor_tensor(out=xn_t[:, k * TC + c_lo: k * TC + c_lo + c_n],
                                                           in0=x_t[:, k, c_lo:c_lo + c_n], scalar=vec(gv, k),
                                                           in1=st_t[:, 0, c_lo:c_lo + c_n], op0=ALU.mult, op1=ALU.mult))

        def prenorm_deferred(gv, cols, pieces, sq_where, want_epsv=False, want_maskr=False):
            c_lo, c_n = cols
            for k in range(KC):
                S.op('act', [r_x(k, c_lo, c_n), R_VECS], [r_xn(k, c_lo, c_n)],
                     lambda e, k=k: e.mul(out=xn_t[:, k * TC + c_lo: k * TC + c_lo + c_n], in_=x_t[:, k, c_lo:c_lo + c_n],
                                          mul=vec(gv, k)))
                o, ro = sq_view(sq_where, k, c_lo, c_n)
                S.op('act', [r_x(k, c_lo, c_n)], [ro],
                     lambda e, k=k, o=o: e.activation(out=o, in_=x_t[:, k, c_lo:c_lo + c_n], func=AF.Square))

            def emit_stats():
                if want_epsv:
                    stats_rstd(sq_where, pieces, cols, 0, keep_sqrt_row=3)
                    S.op('dve', [r_st(3)], [r_st(3)],
                         lambda e: e.scalar_tensor_tensor(out=st_t[:, 3, c_lo:c_lo + c_n], in0=st_t[:, 3, c_lo:c_lo + c_n],
                                                          scalar=EPS, in1=st_t[:, 3, c_lo:c_lo + c_n],
                                                          op0=ALU.mult, op1=ALU.mult))
                else:
                    stats_rstd(sq_where, pieces, cols, 0)
                state['half'] ^= 1
                if want_maskr:
                    S.op('dve', [r_st(0), R_MASK], [r_st(2)],
                         lambda e: e.tensor_tensor(out=st_t[:, 2, c_lo:c_lo + c_n], in0=st_t[:, 0, c_lo:c_lo + c_n],
                                                   in1=maskp_t[:, PADW + c_lo:PADW + c_lo + c_n], op=ALU.mult))
                if want_maskr == 2:
                    S.op('dve', [r_st(0), r_st(2)], [r_st(2)],
                         lambda e: e.tensor_tensor(out=st_t[:, 2, c_lo:c_lo + c_n], in0=st_t[:, 2, c_lo:c_lo + c_n],
                                                   in1=st_t[:, 0, c_lo:c_lo + c_n], op=ALU.mult))
            return emit_stats

        def make_evac_post(gv, where):
            def evac(m, pi, c0, n, ps, rps):
                o, ro = sq_view(where, m, c0, n)
                S.op('act', [rps], [ro], lambda e: e.activation(out=o, in_=ps, func=AF.Square))
                S.op('act', [rps, R_VECS], [r_ht(m, PADW + c0, n)],
                     lambda e: e.mul(out=ht_t[:, m, PADW + c0:PADW + c0 + n], in_=ps, mul=vec(gv, m)))
            return evac

        def postnorm_residual(where, cols, pieces, final=False, epsv_row=None):
            c_lo, c_n = cols
            e0 = PADW + c_lo
            stats_rstd(where, pieces, cols, 1, epsv_row=epsv_row)
            for k in range(KC):
                S.op('dve', [r_ht(k, e0, c_n), r_st(1)], [r_ht(k, e0, c_n)],
                     lambda e, k=k: e.tensor_tensor(out=ht_t[:, k, e0:e0 + c_n], in0=ht_t[:, k, e0:e0 + c_n],
                                                    in1=st_t[:, 1, c_lo:c_lo + c_n], op=ALU.mult))
                if final:
                    S.op('dve', [r_ht(k, e0, c_n), r_x(k, c_lo, c_n)], [r_ht(k, e0, c_n)],
                         lambda e, k=k: e.tensor_tensor(out=ht_t[:, k, e0:e0 + c_n], in0=x_t[:, k, c_lo:c_lo + c_n],
                                                        in1=ht_t[:, k, e0:e0 + c_n], op=ALU.add))
                else:
                    S.op('dve', [r_ht(k, e0, c_n), r_x(k, c_lo, c_n)], [r_x(k, c_lo, c_n)],
                         lambda e, k=k: e.tensor_tensor(out=x_t[:, k, c_lo:c_lo + c_n], in0=x_t[:, k, c_lo:c_lo + c_n],
                                                        in1=ht_t[:, k, e0:e0 + c_n], op=ALU.add))

        def xn_act(k, c0, n):
            return xn_t[:, k * TC + c0: k * TC + c0 + n], r_xn(k, c0, n)

        def window_sum(src_ext, src_reg, w):
            half = w // 2
            cur, cur_r, cur_len = src_ext, src_reg, HTW
            step = 1
            row = 0
            while step < w:
                new_len = cur_len - step
                dst = scr_t[:, row, 0:new_len]
                dreg = r_scr(row, 0, new_len)
                S.op('dve', [cur_r], [dreg],
                     lambda e, cur=cur, dst=dst, step=step, new_len=new_len:
                     e.tensor_tensor(out=dst, in0=cur[:, 0:new_len], in1=cur[:, step:step + new_len], op=ALU.add))
                cur, cur_r, cur_len = dst, dreg, new_len
                step *= 2
                row ^= 1
            off = PADW - half
            return cur[:, off:off + TC], cur_r

        def ffn(l, cols, pieces, final=False):
            c_lo, c_n = cols
            emit_stats = prenorm_deferred(V_FFN_PRE + l, cols, pieces, 'U2', want_epsv=True)
            for fp in range(FC // 2):
                buf = fp % 2

                def evac_gate(m, pi, c0, n, ps, rps, buf=buf):
                    mm = m % 2
                    S.op('dve', [rps, r_st(0)], [r_sgt(buf)],
                         lambda e: e.tensor_tensor(out=sgt_t[:, buf, mm, c0:c0 + n], in0=ps, in1=st_t[:, 0, c0:c0 + n],
                                                   op=ALU.mult))
                    S.op('act', [r_sgt(buf)], [r_sgt(buf)],
                         lambda e: e.activation(out=sgt_t[:, buf, mm, c0:c0 + n], in_=sgt_t[:, buf, mm, c0:c0 + n],
                                                func=AF.Silu))
                proj_fm_one(w_gate[l], fp, xn_act, pieces, evac_gate, before_evac=(emit_stats if fp == 0 else None))

                def evac_up(m, pi, c0, n, ps, rps, buf=buf, fp=fp):
                    mm = m % 2
                    o, ro = U_bf(fp * 2 + mm, c0, n)
                    S.op('dve', [rps, r_sgt(buf)], [ro],
                         lambda e: e.tensor_tensor(out=o, in0=ps, in1=sgt_t[:, buf, mm, c0:c0 + n], op=ALU.mult))
                proj_fm_one(w_up[l], fp, xn_act, pieces, evac_up)
            proj_fm(w_down[l], FC, 4, 0, KC // 2, lambda k, c0, n: U_bf(k, c0, n), pieces,
                    make_evac_post(V_FFN_POST + l, 'xn'))
            postnorm_residual('xn', cols, pieces, final=final, epsv_row=3)

        def proj_fm_one(W2d, mp, act_fn, pieces, evac_fn, before_evac=None):
            banks = take_half(2 * len(pieces))
            slab, sreg, si = next_slab(W2d, 0, KC, mp * 256, 256)
            if S.dry:
                return
            if before_evac is not None:
                for k in range(KC):
                    def emit_k(e, k=k):
                        inst = None
                        for m in range(2):
                            for pi, (c0, n) in enumerate(pieces):
                                b = banks[m * len(pieces) + pi]
                                inst = e.matmul(psb[b][:, 0:n], lhsT=slab[:, k, m * 128:(m + 1) * 128],
                                                rhs=act_fn(k, c0, n)[0], start=(k == 0), stop=(k == KC - 1))
                        return inst
                    S.op('pe', [sreg] + [act_fn(k, c0, n)[1] for (c0, n) in pieces], [r_ps(b) for b in banks], emit_k)
            else:
                reads = [sreg] + [act_fn(k, c0, n)[1] for k in range(KC) for (c0, n) in pieces]

                def emit(e):
                    inst = None
                    for m in range(2):
                        for k in range(KC):
                            for pi, (c0, n) in enumerate(pieces):
                                b = banks[m * len(pieces) + pi]
                                inst = e.matmul(psb[b][:, 0:n], lhsT=slab[:, k, m * 128:(m + 1) * 128],
                                                rhs=act_fn(k, c0, n)[0], start=(k == 0), stop=(k == KC - 1))
                    return inst
                S.op('pe', reads, [r_ps(b) for b in banks], emit)
            after_consume(si)
            if before_evac is not None:
                before_evac()
            for m in range(2):
                for pi, (c0, n) in enumerate(pieces):
                    b = banks[m * len(pieces) + pi]
                    evac_fn(mp * 2 + m, pi, c0, n, psb[b][:, 0:n], r_ps(b))

        FULL = (0, TC)
        PIECES_FULL = [(0, 280), (280, 280)]
        COREC = (HL, CORE)
        PIECES_CORE = [(HL, CORE)]

        def mixer_pool():
            cols, pieces = FULL, PIECES_FULL
            emit_stats = prenorm_deferred(V_MIX_PRE + 0, cols, pieces, 'U', want_maskr=True)
            RC_BASE = 32 * 1120
            for gi, w in enumerate(POOL_WINDOWS):
                sw, sr = window_sum(maskp_t[:, :], R_MASK, w)
                o, ro = U_f32(gi, 0, TC, base=RC_BASE)
                S.op('dve', [sr], [ro], lambda e, sw=sw, o=o: e.tensor_scalar(out=o, in0=sw, scalar1=1.0, scalar2=None, op0=ALU.max))
                S.op('dve', [ro], [ro], lambda e, o=o: e.reciprocal(out=o, in_=o))

            def evac_in(m, pi, c0, n, ps, rps):
                S.op('dve', [rps, r_st(2)], [r_ht(m, PADW + c0, n)],
                     lambda e: e.tensor_tensor(out=ht_t[:, m, PADW + c0:PADW + c0 + n], in0=ps,
                                               in1=st_t[:, 2, c0:c0 + n], op=ALU.mult))
            proj_fm(pool_w_in, KC, 1, 0, KC // 2, xn_act, pieces, evac_in, before_evac=emit_stats)
            for k in range(KC):
                gi = k // 4
                w = POOL_WINDOWS[gi]
                sw, sr = window_sum(ht_t[:, k, :], r_ht(k, 0, HTW), w)
                rc, rcr = U_f32(gi, 0, TC, base=RC_BASE)
                S.op('dve', [sr, rcr], [r_scr(2, 0, TC)],
                     lambda e, sw=sw, rc=rc: e.tensor_tensor(out=scr_t[:, 2, 0:TC], in0=sw, in1=rc, op=ALU.mult))
                o, ro = U_bf(k, 0, TC)
                S.op('dve', [r_scr(2, 0, TC), r_ht(k, PADW, TC)], [ro],
                     lambda e, o=o, k=k: e.tensor_tensor(out=o, in0=scr_t[:, 2, 0:TC], in1=ht_t[:, k, PADW:PADW + TC],
                                                         op=ALU.subtract))
            for g in range(4):
                for mp in range(2):
                    banks = take_half(4)
                    slab, sreg, si = next_slab(pool_w_grp[g], 0, 4, mp * 256, 256)
                    if S.dry:
                        continue
                    reads = [sreg] + [U_bf(g * 4 + kk, c0, n)[1] for kk in range(4) for (c0, n) in pieces]

                    def emit(e, slab=slab, banks=banks, g=g):
                        inst = None
                        for m in range(2):
                            for kk in range(4):
                                for pi, (c0, n) in enumerate(pieces):
                                    inst = e.matmul(psb[banks[m * 2 + pi]][:, 0:n], lhsT=slab[:, kk, m * 128:(m + 1) * 128],
                                                    rhs=U_bf(g * 4 + kk, c0, n)[0], start=(kk == 0), stop=(kk == 3))
                        return inst
                    S.op('pe', reads, [r_ps(b) for b in banks], emit)
                    after_consume(si)
                    for m in range(2):
                        ch = g * 4 + mp * 2 + m
                        for pi, (c0, n) in enumerate(pieces):
                            b = banks[m * 2 + pi]
                            o, ro = U_bf(16 + ch, c0, n)
                            S.op('act', [r_ps(b), R_VECS], [ro],
                                 lambda e, o=o, b=b, n=n, ch=ch: e.mul(out=o, in_=psb[b][:, 0:n], mul=vec(V_POOL_SCALE, ch)))
            proj_fm(pool_w_out, KC, 1, 0, KC // 2, lambda k, c0, n: U_bf(16 + k, c0, n), pieces,
                    make_evac_post(V_MIX_POST + 0, 'xn'))
            postnorm_residual('xn', cols, pieces)

        def mixer_sc():
            cols, pieces = FULL, PIECES_FULL
            emit_stats = prenorm_deferred(V_MIX_PRE + 1, cols, pieces, 'U2', want_epsv=True, want_maskr=2)
            for jp in range(KC // 2):
                buf = jp % 2

                def evac_gc(m, pi, c0, n, ps, rps, buf=buf):
                    mm = m % 2
                    S.op('act', [rps], [r_sgt(buf)],
                         lambda e: e.copy(out=sgt_t[:, buf, mm, c0:c0 + n], in_=ps))
                proj_fm_one(sc_w_in[:, D:2 * D], jp, xn_act, pieces, evac_gc,
                            before_evac=(emit_stats if jp == 0 else None))

                def evac_h(m, pi, c0, n, ps, rps, buf=buf):
                    mm = m % 2
                    S.op('dve', [rps, r_sgt(buf)], [r_scr(mm, 1 + c0, n)],
                         lambda e: e.tensor_tensor(out=scr_t[:, mm, 1 + c0:1 + c0 + n], in0=ps,
                                                   in1=sgt_t[:, buf, mm, c0:c0 + n], op=ALU.mult))
                proj_fm_one(sc_w_in[:, 2 * D:3 * D], jp, xn_act, pieces, evac_h)
                if not S.dry:
                    for mm in range(2):
                        ch = jp * 2 + mm
                        S.op('dve', [], [r_scr(mm, 0, 1)], lambda e, mm=mm: e.memset(scr_t[:, mm, 0:1], 0.0))
                        S.op('dve', [], [r_scr(mm, TC + 1, 1)], lambda e, mm=mm: e.memset(scr_t[:, mm, TC + 1:TC + 2], 0.0))
                        S.op('dve', [r_scr(mm, 1, TC), r_st(2)], [r_scr(mm, 1, TC)],
                             lambda e, mm=mm: e.tensor_tensor(out=scr_t[:, mm, 1:1 + TC], in0=scr_t[:, mm, 1:1 + TC],
                                                              in1=st_t[:, 2, 0:TC], op=ALU.mult))
                        S.op('dve', [r_scr(mm), R_VECS], [r_scr(2 + mm, 0, TC)],
                             lambda e, mm=mm, ch=ch: e.tensor_scalar(out=scr_t[:, 2 + mm, 0:TC], in0=scr_t[:, mm, 1:1 + TC],
                                                                     scalar1=vec(V_SC_CONV + 1, ch), scalar2=None, op0=ALU.mult))
                        for j in (0, 2):
                            S.op('dve', [r_scr(mm), r_scr(2 + mm, 0, TC), R_VECS], [r_scr(2 + mm, 0, TC)],
                                 lambda e, mm=mm, ch=ch, j=j: e.scalar_tensor_tensor(
                                     out=scr_t[:, 2 + mm, 0:TC], in0=scr_t[:, mm, j:j + TC], scalar=vec(V_SC_CONV + j, ch),
                                     in1=scr_t[:, 2 + mm, 0:TC], op0=ALU.mult, op1=ALU.add))

                def evac_gb(m, pi, c0, n, ps, rps):
                    mm = m % 2
                    o, ro = U_bf(m, c0, n)
                    S.op('dve', [rps, r_scr(2 + mm, c0, n)], [ro],
                         lambda e: e.tensor_tensor(out=o, in0=ps, in1=scr_t[:, 2 + mm, c0:c0 + n], op=ALU.mult))
                proj_fm_one(sc_w_in[:, 0:D], jp, xn_act, pieces, evac_gb)
            proj_fm(sc_w_out, KC, 1, 0, KC // 2, lambda k, c0, n: U_bf(k, c0, n), pieces,
                    make_evac_post(V_MIX_POST + 1, 'xn'))
            postnorm_residual('xn', cols, pieces, epsv_row=3)

        def mixer_cf():
            cols, pieces = FULL, PIECES_FULL
            emit_stats = prenorm_deferred(V_MIX_PRE + 2, cols, pieces, 'U2', want_maskr=True)
            CIW = 592
            HB0 = 2816
            SQ0 = HB0 + KC * CORE

            def ci_bf(par, mm, c0, n):
                off = (par * 2 + mm) * CIW + c0
                return U_t[:, off:off + n], r_U(off * 2, (off + n) * 2)

            def hb_v(k):
                off = HB0 + k * CORE
                return U_t[:, off:off + CORE], r_U(off * 2, (off + CORE) * 2)

            def sq_v(k):
                off = SQ0 + k * CORE
                return U_t[:, off:off + CORE], r_U(off * 2, (off + CORE) * 2)
            e0 = PADW + HL

            def conv_pe(jp):
                par = jp % 2
                banks = take_half(2)
                for mm in range(2):
                    ch = jp * 2 + mm
                    slab, sreg, si = next_slab(dgw[ch], 0, 1, 0, 31 * 128)
                    if S.dry:
                        continue
                    b = banks[mm]

                    def emit(e, slab=slab, b=b, mm=mm):
                        inst = None
                        for j in range(31):
                            inst = e.matmul(psb[b][:, 0:CORE], lhsT=slab[:, 0, j * 128:(j + 1) * 128],
                                            rhs=ci_bf(par, mm, HL + j, CORE)[0], start=(j == 0), stop=(j == 30))
                        return inst
                    S.op('pe', [sreg, ci_bf(par, mm, 0, CIW)[1]], [r_ps(b)], emit)
                    after_consume(si)
                    S.op('act', [r_ps(b), R_VECS], [r_ht(ch, e0, CORE)],
                         lambda e, b=b, ch=ch: e.activation(out=ht_t[:, ch, e0:e0 + CORE], in_=psb[b][:, 0:CORE],
                                                            func=AF.Identity, bias=vec(V_CF_DWB, ch)))
                    o1, r1 = hb_v(ch)
                    S.op('act', [r_ps(b), R_VECS], [r1],
                         lambda e, b=b, ch=ch, o1=o1: e.activation(out=o1, in_=psb[b][:, 0:CORE], func=AF.Identity,
                                                                   bias=vec(V_CF_DWB, ch)))
                    o2, r2 = sq_v(ch)
                    S.op('act', [r_ps(b), R_VECS], [r2],
                         lambda e, b=b, ch=ch, o2=o2: e.activation(out=o2, in_=psb[b][:, 0:CORE], func=AF.Square,
                                                                   bias=vec(V_CF_DWB, ch)))

            for jp in range(KC // 2):
                buf = jp % 2
                par = jp % 2

                def evac_gate(m, pi, c0, n, ps, rps, buf=buf):
                    mm = m % 2
                    S.op('dve', [rps, r_st(0)], [r_sgt(buf)],
                         lambda e: e.tensor_tensor(out=sgt_t[:, buf, mm, c0:c0 + n], in0=ps, in1=st_t[:, 0, c0:c0 + n],
                                                   op=ALU.mult))
                    S.op('act', [r_sgt(buf)], [r_sgt(buf)],
                         lambda e: e.activation(out=sgt_t[:, buf, mm, c0:c0 + n], in_=sgt_t[:, buf, mm, c0:c0 + n],
                                                func=AF.Sigmoid))
                proj_fm_one(cf_w_in[:, D:2 * D], jp, xn_act, pieces, evac_gate,
                            before_evac=(emit_stats if jp == 0 else None))

                def evac_a(m, pi, c0, n, ps, rps, buf=buf, par=par):
                    mm = m % 2
                    o, ro = ci_bf(par, mm, 15 + c0, n)
                    S.op('dve', [rps, r_sgt(buf)], [ro],
                         lambda e: e.tensor_tensor(out=o, in0=ps, in1=sgt_t[:, buf, mm, c0:c0 + n], op=ALU.mult))
                proj_fm_one(cf_w_in[:, 0:D], jp, xn_act, pieces, evac_a)
                if not S.dry:
                    for mm in range(2):
                        cm, cmr = ci_bf(par, mm, 15, TC)
                        S.op('dve', [cmr, r_st(2)], [cmr],
                             lambda e, cm=cm: e.tensor_tensor(out=cm, in0=cm, in1=st_t[:, 2, 0:TC], op=ALU.mult))
                if jp >= 1:
                    conv_pe(jp - 1)
            conv_pe(KC // 2 - 1)
            if S.dry:
                proj_fm(cf_w_out, KC, 1, 0, KC // 2, None, PIECES_CORE, None)
                return
            cols, pieces = COREC, PIECES_CORE
            bks = take_half(2)

            def emit_s(e):
                inst = None
                for which, fn in enumerate((hb_v, sq_v)):
                    for k in range(KC):
                        inst = e.matmul(psb[bks[which]][:, 0:CORE], lhsT=ones_b[:, :], rhs=fn(k)[0],
                                        start=(k == 0), stop=(k == KC - 1))
                return inst
            S.op('pe', [R_CONST, r_U(HB0 * 2, (SQ0 + KC * CORE) * 2)], [r_ps(b) for b in bks], emit_s)
            sl = slice(HL, HL + CORE)
            S.op('act', [r_ps(bks[0])], [r_st(2)],
                 lambda e: e.mul(out=st_t[:, 2, sl], in_=psb[bks[0]][:, 0:CORE], mul=1.0 / D))
            S.op('act', [r_ps(bks[1])], [r_st(1)],
                 lambda e: e.mul(out=st_t[:, 1, sl], in_=psb[bks[1]][:, 0:CORE], mul=1.0 / D))
            S.op('dve', [r_st(2)], [r_st(0)],
                 lambda e: e.tensor_tensor(out=st_t[:, 0, sl], in0=st_t[:, 2, sl], in1=st_t[:, 2, sl], op=ALU.mult))
            S.op('dve', [r_st(1), r_st(0)], [r_st(1)],
                 lambda e: e.tensor_tensor(out=st_t[:, 1, sl], in0=st_t[:, 1, sl], in1=st_t[:, 0, sl], op=ALU.subtract))
            S.op('dve', [r_st(1)], [r_st(1)],
                 lambda e: e.tensor_scalar(out=st_t[:, 1, sl], in0=st_t[:, 1, sl], scalar1=0.0, scalar2=EPS,
                                           op0=ALU.max, op1=ALU.add))
            S.op('act', [r_st(1)], [r_st(1)], lambda e: e.activation(out=st_t[:, 1, sl], in_=st_t[:, 1, sl], func=AF.Sqrt))
            S.op('dve', [r_st(1)], [r_st(1)], lambda e: e.reciprocal(out=st_t[:, 1, sl], in_=st_t[:, 1, sl]))
            for k in range(KC):
                S.op('dve', [r_ht(k, e0, CORE), r_st(2)], [r_ht(k, e0, CORE)],
                     lambda e, k=k: e.tensor_tensor(out=ht_t[:, k, e0:e0 + CORE], in0=ht_t[:, k, e0:e0 + CORE],
                                                    in1=st_t[:, 2, sl], op=ALU.subtract))
                S.op('dve', [r_ht(k, e0, CORE), r_st(1)], [r_ht(k, e0, CORE)],
                     lambda e, k=k: e.tensor_tensor(out=ht_t[:, k, e0:e0 + CORE], in0=ht_t[:, k, e0:e0 + CORE],
                                                    in1=st_t[:, 1, sl], op=ALU.mult))
                o, ro = U_bf(k, HL, CORE)
                S.op('act', [r_ht(k, e0, CORE), R_VECS], [ro],
                     lambda e, k=k, o=o: e.activation(out=o, in_=ht_t[:, k, e0:e0 + CORE], func=AF.Silu,
                                                      bias=vec(V_CF_LNB, k), scale=vec(V_CF_LNG, k)))
            proj_fm(cf_w_out, KC, 1, 0, KC // 2, lambda k, c0, n: U_bf(k, c0, n), pieces,
                    make_evac_post(V_MIX_POST + 2, 'xn'))
            postnorm_residual('xn', cols, pieces)

        def mixer_sg():
            cols, pieces = COREC, PIECES_CORE
            emit_stats_fm = prenorm_deferred(V_MIX_PRE + 3, cols, pieces, 'U2')

            def emit_stats():
                emit_stats_fm()
                bk = take_half(1)
                state['half'] ^= 1
                reads = [R_CONST] + [sq_view('U2', k, HL, CORE)[1] for k in range(KC)]

                def emit_c(e):
                    inst = None
                    for tc_ in range(4):
                        for k in range(KC):
                            inst = e.matmul(psb[bk[0]][:, tc_:tc_ + 1], lhsT=sq_view('U2', k, HL + tc_ * 128, 128)[0],
                                            rhs=ones_b[:, 0:1], start=(k == 0), stop=(k == KC - 1))
                    return inst
                S.op('pe', reads, [r_ps(bk[0])], emit_c)
                S.op('act', [r_ps(bk[0]), r_tiny(0)], [r_tiny(40, 4)],
                     lambda e: e.activation(out=tiny_t[:, 40:44], in_=psb[bk[0]][:, 0:4], func=AF.Sqrt,
                                            bias=tiny_t[:, 0:1], scale=1.0 / D))
                S.op('dve', [r_tiny(40, 4)], [r_tiny(40, 4)],
                     lambda e: e.reciprocal(out=tiny_t[:, 40:44], in_=tiny_t[:, 40:44]))

            def evac_u(m, pi, c0, n, ps, rps):
                S.op('dve', [rps, r_st(0)], [r_ht(m, PADW + c0, n)],
                     lambda e: e.tensor_tensor(out=ht_t[:, m, PADW + c0:PADW + c0 + n], in0=ps, in1=st_t[:, 0, c0:c0 + n],
                                               op=ALU.mult))
                S.op('act', [r_ht(m, PADW + c0, n)], [r_ht(m, PADW + c0, n)],
                     lambda e: e.activation(out=ht_t[:, m, PADW + c0:PADW + c0 + n], in_=ht_t[:, m, PADW + c0:PADW + c0 + n],
                                            func=AF.Gelu))

            def vtok(tc_, f0, n):
                b0 = (tc_ * D + f0) * 4
                return U_t[:, b0 // 2: b0 // 2 + 2 * n].bitcast(F32), r_U(b0, b0 + 4 * n)

            def vln(tc_, f0, n):
                b0 = 32768 + (tc_ * D + f0) * 2
                return U_t[:, b0 // 2: b0 // 2 + n], r_U(b0, b0 + 2 * n)
            for jp in range(KC // 2):
                banks = take_half(4)
                slab, sreg, si = next_slab(sg_w_in, 0, KC, D + jp * 256, 256)
                if S.dry:
                    continue
                if jp == 0:
                    for k in range(KC):
                        def emit_k(e, slab=slab, banks=banks, k=k):
                            inst = None
                            for tc_ in range(4):
                                c = k * TC + HL + tc_ * 128
                                inst = e.matmul(psb[banks[tc_]][:, 0:256], lhsT=xn_t[:, c:c + 128], rhs=slab[:, k, 0:256],
                                                start=(k == 0), stop=(k == KC - 1))
                            return inst
                        S.op('pe', [sreg, r_xn(k, HL, CORE)], [r_ps(b) for b in banks], emit_k)
                else:
                    reads = [sreg] + [r_xn(k, HL, CORE) for k in range(KC)]

                    def emit(e, slab=slab, banks=banks):
                        inst = None
                        for tc_ in range(4):
                            for k in range(KC):
                                c = k * TC + HL + tc_ * 128
                                inst = e.matmul(psb[banks[tc_]][:, 0:256], lhsT=xn_t[:, c:c + 128], rhs=slab[:, k, 0:256],
                                                start=(k == 0), stop=(k == KC - 1))
                        return inst
                    S.op('pe', reads, [r_ps(b) for b in banks], emit)
                after_consume(si)
                if jp == 0:
                    emit_stats()
                for tc_ in range(4):
                    o, ro = vtok(tc_, jp * 256, 256)
                    S.op('act', [r_ps(banks[tc_]), r_tiny(40 + tc_)], [ro],
                         lambda e, o=o, b=banks[tc_], tc_=tc_: e.activation(out=o, in_=psb[b][:, 0:256], func=AF.Gelu,
                                                                            scale=tiny_t[:, 40 + tc_:41 + tc_]))
            proj_fm(sg_w_in, KC, 1, 0, KC // 2, xn_act, pieces, evac_u)
            if not S.dry:
                for tc_ in range(4):
                    v_ap, v_r = vtok(tc_, 0, D)
                    j_ap, j_r = vln(tc_, 0, D)
                    c = 8 + tc_ * 8
                    S.op('dve', [v_r], [r_tiny(c)], lambda e, v_ap=v_ap, c=c: e.reduce_sum(out=tiny_t[:, c:c + 1], in_=v_ap, axis=AX.X))
                    S.op('dve', [], [r_tiny(c + 1)], lambda e, c=c: e.memset(tiny_t[:, c + 1:c + 2], 0.0))
                    S.op('act', [v_r, r_tiny(c + 1)], [j_r, r_tiny(c + 1)],
                         lambda e, v_ap=v_ap, j_ap=j_ap, c=c: e.activation(out=j_ap, in_=v_ap, func=AF.Square,
                                                                           accum_out=tiny_t[:, c + 1:c + 2]))
                    S.op('dve', [r_tiny(c, 2)], [r_tiny(c, 2)],
                         lambda e, c=c: e.tensor_scalar(out=tiny_t[:, c:c + 2], in0=tiny_t[:, c:c + 2], scalar1=1.0 / D,
                                                        scalar2=None, op0=ALU.mult))
                    S.op('dve', [r_tiny(c)], [r_tiny(c + 2)],
                         lambda e, c=c: e.tensor_tensor(out=tiny_t[:, c + 2:c + 3], in0=tiny_t[:, c:c + 1],
                                                        in1=tiny_t[:, c:c + 1], op=ALU.mult))
                    S.op('dve', [r_tiny(c + 1, 2)], [r_tiny(c + 1)],
                         lambda e, c=c: e.tensor_tensor(out=tiny_t[:, c + 1:c + 2], in0=tiny_t[:, c + 1:c + 2],
                                                        in1=tiny_t[:, c + 2:c + 3], op=ALU.subtract))
                    S.op('dve', [r_tiny(c + 1)], [r_tiny(c + 1)],
                         lambda e, c=c: e.tensor_scalar(out=tiny_t[:, c + 1:c + 2], in0=tiny_t[:, c + 1:c + 2], scalar1=0.0,
                                                        scalar2=EPS, op0=ALU.max, op1=ALU.add))
                    S.op('act', [r_tiny(c + 1)], [r_tiny(c + 1)],
                         lambda e, c=c: e.activation(out=tiny_t[:, c + 1:c + 2], in_=tiny_t[:, c + 1:c + 2], func=AF.Sqrt))
                    S.op('dve', [r_tiny(c + 1)], [r_tiny(c + 1)],
                         lambda e, c=c: e.reciprocal(out=tiny_t[:, c + 1:c + 2], in_=tiny_t[:, c + 1:c + 2]))
                    S.op('dve', [v_r, r_tiny(c, 2)], [j_r],
                         lambda e, v_ap=v_ap, j_ap=j_ap, c=c: e.tensor_scalar(out=j_ap, in0=v_ap, scalar1=tiny_t[:, c:c + 1],
                                                                              scalar2=tiny_t[:, c + 1:c + 2],
                                                                              op0=ALU.subtract, op1=ALU.mult))
                for kq in range(4):
                    banks = take_half(4)
                    reads = [R_CONST] + [vln(tc_, kq * 512, 512)[1] for tc_ in range(4)]

                    def emit(e, kq=kq, banks=banks):
                        inst = None
                        for kk in range(4):
                            k = kq * 4 + kk
                            for tc_ in range(4):
                                inst = e.matmul(psb[banks[kk]][:, tc_ * 128:(tc_ + 1) * 128],
                                                lhsT=vln(tc_, k * 128, 128)[0], rhs=wsT_t[:, k // 2, :],
                                                start=True, stop=True)
                        return inst
                    S.op('pe', reads, [r_ps(b) for b in banks], emit)
                    for kk in range(4):
                        k = kq * 4 + kk
                        h = k // 2
                        b = banks[kk]
                        S.op('dve', [R_R, R_CONST, R_VECS], [r_scr(0, 0, 128)],
                             lambda e, k=k, h=h: e.scalar_tensor_tensor(out=scr_t[:, 0, 0:128], in0=R_t[:, h, :],
                                                                        scalar=vec(V_SG_LNB, k), in1=bsb_t[:, h, :],
                                                                        op0=ALU.mult, op1=ALU.add))
                        for tc_ in range(4):
                            S.op('dve', [r_ps(b), r_scr(0, 0, 128), R_VECS], [r_scr(1, tc_ * 128, 128)],
                                 lambda e, k=k, b=b, tc_=tc_: e.scalar_tensor_tensor(
                                     out=scr_t[:, 1, tc_ * 128:(tc_ + 1) * 128], in0=psb[b][:, tc_ * 128:(tc_ + 1) * 128],
                                     scalar=vec(V_SG_LNG, k), in1=scr_t[:, 0, 0:128], op0=ALU.mult, op1=ALU.add))
                        S.op('dve', [r_scr(1, 0, 512), r_ht(k, PADW + HL, CORE)], [r_xn(k, HL, CORE)],
                             lambda e, k=k: e.tensor_tensor(out=xn_t[:, k * TC + HL:k * TC + HL + CORE], in0=scr_t[:, 1, 0:512],
                                                            in1=ht_t[:, k, PADW + HL:PADW + HL + CORE], op=ALU.mult))
            proj_fm(sg_w_out, KC, 1, 0, KC // 2, xn_act, pieces, make_evac_post(V_MIX_POST + 3, 'U'))
            postnorm_residual('U', cols, pieces)

        def load_x(i):
            for k in range(KC):
                S.dma('sp', f'xl{k}', [], [r_x(k, 0, TC)],
                      lambda e, k=k: e.dma_start(out=x_t[:, k, :], in_=xin[i][k * 128:(k + 1) * 128, :]))

        def load_mask(i):
            S.dma('sp', 'mld', [], [R_MASK],
                  lambda e: e.dma_start(out=maskp_t[:, PADW:PADW + TC], in_=maskd[i]))

        def tile_body(i):
            if i == 0 and not S.dry:
                load_x(0)
                load_mask(0)
            mixer_pool()
            ffn(0, FULL, PIECES_FULL)
            mixer_sc()
            ffn(1, FULL, PIECES_FULL)
            mixer_cf()
            if i + 1 < NT and not S.dry:
                load_mask(i + 1)
            ffn(2, COREC, PIECES_CORE)
            mixer_sg()
            ffn(3, COREC, PIECES_CORE, final=True)
            if not S.dry:
                if i + 1 < NT:
                    load_x(i + 1)
                S.dma('sp', 'yst', [('ht', 0, KC * HTW * 4)], [],
                      lambda e: e.dma_start(out=yout[i].rearrange("(k p) t -> p k t", p=128),
                                            in_=ht_t[:, :, PADW + HL:PADW + HL + CORE]))

        def prologue():
            S.dma('sp', 'cst', [], [R_VECS], lambda e: e.dma_start(out=vecs_t[:, :, :].rearrange("p v k -> p (v k)"), in_=vecsd))
            S.dma('sp', 'cst', [], [R_CONST], lambda e: e.dma_start(out=bsb_t[:, :, :].rearrange("p h q -> p (h q)"), in_=bsbd))
            S.dma('pool', 'cst', [], [R_CONST], lambda e: e.dma_start(out=wsT_t[:, :, :].rearrange("p h q -> p (h q)"), in_=wsTd))
            S.dma('pool', 'cst', [], [R_CONST], lambda e: e.dma_start(out=ident_b[:, :], in_=identd))
            S.op('dve', [], [R_CONST], lambda e: e.memset(ones_f[:, :], 1.0))
            S.op('dve', [], [R_CONST], lambda e: e.memset(ones_b[:, :], 1.0))
            S.op('dve', [], [r_tiny(0)], lambda e: e.memset(tiny_t[:, 0:1], EPS))
            S.op('dve', [], [R_MASK], lambda e: e.memset(maskp_t[:, :], 0.0))
            S.op('dve', [], [('ht', 0, KC * HTW * 4)], lambda e: e.memset(ht_t[:, :, :].rearrange("p k t -> p (k t)"), 0.0))
            S.op('dve', [], [r_scr(0), r_scr(1), r_scr(2), r_scr(3)],
                 lambda e: e.memset(scr_t[:, :, :].rearrange("p k t -> p (k t)"), 0.0))
            for hb in range(2):
                banks = take_half(1)

                def emit(e, hb=hb, banks=banks):
                    inst = None
                    for hh in range(4):
                        inst = e.matmul(psb[banks[0]][:, hh * 128:(hh + 1) * 128], lhsT=ones_b[:, :],
                                        rhs=wsT_t[:, hb * 4 + hh, :], start=True, stop=True)
                    return inst
                S.op('pe', [R_CONST], [r_ps(banks[0])], emit)
                S.op('act', [r_ps(banks[0])], [R_R],
                     lambda e, hb=hb, banks=banks: e.activation(out=R_t[:, hb * 4:(hb + 1) * 4, :].rearrange("p h q -> p (h q)"),
                                                                in_=psb[banks[0]][:, 0:512], func=AF.Identity))

        S.dry = True
        for i in range(NT):
            tile_body(i)
        S.dry = False
        state['half'] = 0
        prologue()
        for i in range(NT):
            tile_body(i)
        assert state['next_consume'] == len(plan), (state['next_consume'], len(plan))
        S.final_wait('sp', ['yst'])

        with nc.Block() as block:
            @block.tensor
            def _(e):
                for f in S.prog['pe']:
                    f(e)

            @block.scalar
            def _(e):
                for f in S.prog['act']:
                    f(e)

            @block.vector
            def _(e):
                for f in S.prog['dve']:
                    f(e)

            @block.gpsimd
            def _(e):
                for f in S.prog['pool']:
                    f(e)

            @block.sync
            def _(e):
                for f in S.prog['sp']:
                    f(e)
        print("program sizes", {k: len(v) for k, v in S.prog.items()}, "slabs", len(plan), flush=True)
    return nc


def _tiles_layout(seqs, n_cores, NT):
    tiles = []
    for si, a in enumerate(seqs):
        assert a.shape[0] % CORE == 0
        for s0 in range(0, a.shape[0], CORE):
            tiles.append((si, s0))
    assert len(tiles) == n_cores * NT, (len(tiles), n_cores, NT)
    return tiles


def run_trunk_device(seqs, W, n_cores, NT):
    tiles = _tiles_layout(seqs, n_cores, NT)
    xin = np.zeros((n_cores, NT, D, TC), np.float32)
    maskb = np.zeros((n_cores, NT, 128, TC), np.float32)
    for t, (si, s0) in enumerate(tiles):
        c, i = divmod(t, NT)
        a = seqs[si]
        Sl = a.shape[0]
        lo, hi = max(0, s0 - HL), min(Sl, s0 + CORE + HL)
        o0 = lo - (s0 - HL)
        xin[c, i, :, o0:o0 + (hi - lo)] = a[lo:hi, :].T
        maskb[c, i, :, o0:o0 + (hi - lo)] = 1.0

    def pk(v):
        return np.asarray(v, np.float32).reshape(KC, 128).T
    vecs = np.zeros((128, NV, KC), np.float32)
    for l in range(4):
        vecs[:, V_MIX_PRE + l] = pk(W["mix_pre_g"][l])
        vecs[:, V_MIX_POST + l] = pk(W["mix_post_g"][l])
        vecs[:, V_FFN_PRE + l] = pk(W["ffn_pre_g"][l])
        vecs[:, V_FFN_POST + l] = pk(W["ffn_post_g"][l])
    vecs[:, V_POOL_SCALE] = pk(W["pool_scale"][0])
    vecs[:, V_CF_DWB] = pk(W["cf_dw_b"][0])
    vecs[:, V_CF_LNG] = pk(W["cf_ln_g"][0])
    vecs[:, V_CF_LNB] = pk(W["cf_ln_b"][0])
    vecs[:, V_SG_LNG] = pk(W["sg_ln_g"][0])
    vecs[:, V_SG_LNB] = pk(W["sg_ln_b"][0])
    for j in range(3):
        vecs[:, V_SC_CONV + j] = pk(W["sc_conv_w"][0, j])
    for j in range(31):
        vecs[:, V_CF_DW + j] = pk(W["cf_dw_w"][0, j])
    vecs = np.ascontiguousarray(vecs.reshape(128, NV * KC))
    dgw = np.zeros((KC, 128, 31, 128), np.float32)
    cw = np.asarray(W["cf_dw_w"][0], np.float32).reshape(31, KC, 128)
    pidx = np.arange(128)
    for j in range(31):
        dgw[:, pidx, j, pidx] = cw[j]
    dgw = dgw.reshape(KC, 128, 31 * 128)
    wsT = np.ascontiguousarray(np.asarray(W["sg_w_s"][0], np.float32).transpose(2, 0, 1).reshape(128, 8 * 128))
    bsb = np.ascontiguousarray(np.broadcast_to(np.asarray(W["sg_b_s"][0], np.float32).reshape(1, 8 * 128), (128, 8 * 128)))

    shared = {
        "vecs": vecs, "wsT": wsT, "bsb": bsb, "ident": np.eye(128, dtype=np.float32), "dgw": dgw,
        "ffn_w_gate": np.asarray(W["ffn_w_gate"], np.float32),
        "ffn_w_up": np.asarray(W["ffn_w_up"], np.float32),
        "ffn_w_down": np.asarray(W["ffn_w_down"], np.float32),
        "pool_w_in": np.asarray(W["pool_w_in"][0], np.float32),
        "pool_w_grp": np.asarray(W["pool_w_grp"][0], np.float32),
        "pool_w_out": np.asarray(W["pool_w_out"][0], np.float32),
        "sc_w_in": np.asarray(W["sc_w_in"][0], np.float32),
        "sc_w_out": np.asarray(W["sc_w_out"][0], np.float32),
        "cf_w_in": np.asarray(W["cf_w_in"][0], np.float32),
        "cf_w_out": np.asarray(W["cf_w_out"][0], np.float32),
        "sg_w_in": np.asarray(W["sg_w_in"][0], np.float32),
        "sg_w_out": np.asarray(W["sg_w_out"][0], np.float32),
    }
    nc = build_program(NT)
    in_maps = []
    for c in range(n_cores):
        m = dict(shared)
        m["xin"] = xin[c]
        m["maskb"] = maskb[c]
        in_maps.append(m)
    res = run_bass_kernel_spmd(nc, in_maps, core_ids=list(range(n_cores)))
    outs = [np.empty_like(a) for a in seqs]
    for t, (si, s0) in enumerate(tiles):
        c, i = divmod(t, NT)
        outs[si][s0:s0 + CORE, :] = res.results[c]["yout"][i].T
    return outs


def kernel(**inputs):
    xp = np.asarray(inputs["x_prompt"], np.float32)
    xs = np.asarray(inputs["x_sample"], np.float32)
    seqs = [xp[b] for b in range(xp.shape[0])] + [xs[b] for b in range(xs.shape[0])]
    outs = run_trunk_device(seqs, inputs, 8, 12)
    y_prompt = np.stack(outs[:xp.shape[0]], axis=0)
    y_sample = np.stack(outs[xp.shape[0]:], axis=0)
    return (y_prompt, y_sample)
```

```python
import numpy as np
import concourse.bass as bass
import concourse.mybir as mybir
from concourse.bass_utils import run_bass_kernel_spmd

F32 = mybir.dt.float32
BF16 = mybir.dt.bfloat16
AF = mybir.ActivationFunctionType
ALU = mybir.AluOpType
AX = mybir.AxisListType

D = 2048
KC = 16
DFF = 5632
FC = 44
HL = 24
CORE = 512
TC = CORE + 2 * HL
PADW = 8
HTW = TC + 2 * PADW
EPS = 1e-6
NSLOT = 3
SLOT_ELEMS = 4096
NV = 56
POOL_WINDOWS = (2, 4, 8, 16)
SCRW = 592

V_MIX_PRE, V_MIX_POST, V_FFN_PRE, V_FFN_POST = 0, 4, 8, 12
V_POOL_SCALE, V_CF_DWB, V_CF_LNG, V_CF_LNB, V_SG_LNG, V_SG_LNB = 16, 17, 18, 19, 20, 21
V_SC_CONV, V_CF_DW = 22, 25


class Sched:
    def __init__(self, engs, semh):
        self.E = engs
        self.semh = semh
        self.cnt = {k: 0 for k in semh}
        self.known = {e: {} for e in engs}
        self.cells = {}
        self.cellsz = {}
        self.dry = False
        self.prog = {e: [] for e in engs}

    def alloc(self, name, cell_bytes):
        self.cellsz[name] = cell_bytes

    def _cells(self, regs):
        for (a, b0, b1) in regs:
            cs = self.cellsz[a]
            for c in range(b0 // cs, (b1 - 1) // cs + 1):
                yield (a, c)

    def _gather(self, reads, writes):
        deps = {}

        def add(k, v):
            if deps.get(k, 0) < v:
                deps[k] = v
        for c in self._cells(reads):
            st = self.cells.get(c)
            if st and st[0] is not None:
                add(*st[0])
        for c in self._cells(writes):
            st = self.cells.get(c)
            if st:
                if st[0] is not None:
                    add(*st[0])
                for k, v in st[1].items():
                    add(k, v)
        return deps

    def _record(self, tok, reads, writes):
        k, v = tok
        for c in self._cells(writes):
            self.cells[c] = [tok, {}]
        for c in self._cells(reads):
            st = self.cells.setdefault(c, [None, {}])
            if st[1].get(k, 0) < v:
                st[1][k] = v

    def _waits(self, eng, deps):
        kn = self.known[eng]
        for k, v in deps.items():
            if k == eng and eng == 'pe':
                continue
            if kn.get(k, 0) >= v:
                continue
            self.prog[eng].append(lambda e, s=self.semh[k], v=v: e.wait_ge(s, v))
            kn[k] = v

    def op(self, eng, reads, writes, emit):
        if self.dry:
            return
        self._waits(eng, self._gather(reads, writes))
        self.cnt[eng] += 1
        self.prog[eng].append(lambda e, emit=emit, s=self.semh[eng]: emit(e).then_inc(s, 1))
        self._record((eng, self.cnt[eng]), reads, writes)

    def dma(self, eng, semname, reads, writes, emit):
        if self.dry:
            return
        self._waits(eng, self._gather(reads, writes))
        self.cnt[semname] += 16
        self.prog[eng].append(lambda e, emit=emit, s=self.semh[semname]: emit(e).then_inc(s, 16))
        self._record((semname, self.cnt[semname]), reads, writes)

    def final_wait(self, eng, keys):
        for k in keys:
            if self.cnt[k] > 0:
                self.prog[eng].append(lambda e, s=self.semh[k], v=self.cnt[k]: e.wait_ge(s, v))


def build_program(NT):
    nc = bass.Bass("TRN2", target_bir_lowering=False)

    def din(name, shape):
        return nc.dram_tensor(name, list(shape), F32, kind="ExternalInput").ap()

    xin = din("xin", [NT, D, TC])
    maskd = din("maskb", [NT, 128, TC])
    vecsd = din("vecs", [128, NV * KC])
    wsTd = din("wsT", [128, 8 * 128])
    bsbd = din("bsb", [128, 8 * 128])
    identd = din("ident", [128, 128])
    dgw = din("dgw", [KC, 128, 31 * 128])
    w_gate = din("ffn_w_gate", [4, D, DFF])
    w_up = din("ffn_w_up", [4, D, DFF])
    w_down = din("ffn_w_down", [4, DFF, D])
    pool_w_in = din("pool_w_in", [D, D])
    pool_w_grp = din("pool_w_grp", [4, 512, 512])
    pool_w_out = din("pool_w_out", [D, D])
    sc_w_in = din("sc_w_in", [D, 3 * D])
    sc_w_out = din("sc_w_out", [D, D])
    cf_w_in = din("cf_w_in", [D, 2 * D])
    cf_w_out = din("cf_w_out", [D, D])
    sg_w_in = din("sg_w_in", [D, 2 * D])
    sg_w_out = din("sg_w_out", [D, D])
    yout = nc.dram_tensor("yout", [NT, D, CORE], F32, kind="ExternalOutput").ap()

    from contextlib import ExitStack
    with ExitStack() as es:
        def sb(name, shape, dt):
            return es.enter_context(nc.sbuf_tensor(name, list(shape), dt))

        x_t = sb("x_t", [128, KC, TC], F32)
        xn_t = sb("xn_t", [128, KC * TC], BF16)
        ht_t = sb("ht_t", [128, KC, HTW], F32)
        U_t = sb("U_t", [128, FC * TC], BF16)
        ring_t = sb("ring_t", [128, NSLOT, SLOT_ELEMS], BF16)
        vecs_t = sb("vecs_t", [128, NV, KC], F32)
        maskp_t = sb("maskp_t", [128, HTW], F32)
        sgt_t = sb("sgt_t", [128, 2, 2, TC], F32)
        scr_t = sb("scr_t", [128, 4, SCRW], F32)
        st_t = sb("st_t", [128, 4, TC], F32)
        wsT_t = sb("wsT_t", [128, 8, 128], BF16)
        bsb_t = sb("bsb_t", [128, 8, 128], F32)
        R_t = sb("R_t", [128, 8, 128], F32)
        ones_f = sb("ones_f", [128, 128], F32)
        ones_b = sb("ones_b", [128, 128], BF16)
        ident_b = sb("ident_b", [128, 128], BF16)
        tiny_t = sb("tiny_t", [128, 64], F32)
        psb = [es.enter_context(nc.psum_tensor(f"ps{b}", [128, 512], F32)) for b in range(8)]

        semnames = ['pe', 'act', 'dve', 'pool', 'sp', 'mld', 'yst', 'cst', 'wcp'] + [f'xl{k}' for k in range(KC)] + [f'slot{s}' for s in range(NSLOT)]
        semh = {k: es.enter_context(nc.semaphore(k)) for k in semnames}
        engs = {k: None for k in ('pe', 'act', 'dve', 'pool', 'sp')}
        S = Sched(engs, semh)
        S.alloc('x', 1120)
        S.alloc('xn', 560)
        S.alloc('ht', HTW * 4)
        S.alloc('U', 560)
        S.alloc('ring', SLOT_ELEMS * 2)
        S.alloc('vecs', 1 << 20)
        S.alloc('mask', 1 << 20)
        S.alloc('sgt', 2 * TC * 4)
        S.alloc('scr', SCRW * 4)
        S.alloc('st', TC * 4)
        S.alloc('const', 1 << 20)
        S.alloc('R', 1 << 20)
        S.alloc('tiny', 4)
        S.alloc('ps', 2048)
        S.alloc('wscr', 1)

        def r_x(k, c0, n): return ('x', (k * TC + c0) * 4, (k * TC + c0 + n) * 4)
        def r_xn(k, c0, n): return ('xn', (k * TC + c0) * 2, (k * TC + c0 + n) * 2)
        def r_ht(k, e0, n): return ('ht', (k * HTW + e0) * 4, (k * HTW + e0 + n) * 4)
        def r_U(b0, b1): return ('U', b0, b1)
        def r_ps(b): return ('ps', b * 2048, (b + 1) * 2048)
        def r_scr(r, c0=0, n=SCRW): return ('scr', (r * SCRW + c0) * 4, (r * SCRW + c0 + n) * 4)
        def r_st(r): return ('st', r * TC * 4, (r + 1) * TC * 4)
        def r_sgt(b): return ('sgt', b * 2 * TC * 4, (b + 1) * 2 * TC * 4)
        def r_tiny(c, n=1): return ('tiny', c * 4, (c + n) * 4)
        R_VECS = ('vecs', 0, 1)
        R_MASK = ('mask', 0, 1)
        R_CONST = ('const', 0, 1)
        R_R = ('R', 0, 1)

        def U_bf(k, c0, n):
            return U_t[:, k * TC + c0: k * TC + c0 + n], r_U((k * TC + c0) * 2, (k * TC + c0 + n) * 2)

        def U_f32(k, c0, n, base=0):
            b0 = base + (k * TC + c0) * 4
            return (U_t[:, b0 // 2: b0 // 2 + 2 * n].bitcast(F32), r_U(b0, b0 + 4 * n))

        def vec(v, k):
            return vecs_t[:, v, k:k + 1]

        plan = []
        state = {'next_consume': 0, 'next_issue': 0, 'half': 0}

        def issue_slab(i):
            W2d, k0, nk, c0, ncol = plan[i]
            s = i % NSLOT
            spt = state['spt']
            j = i % spt
            rreg = ('ring', s * SLOT_ELEMS * 2, (s + 1) * SLOT_ELEMS * 2)
            if i < spt or j >= state['wch']:
                src = W2d[k0 * 128:(k0 + nk) * 128, c0:c0 + ncol].rearrange("(k p) c -> p k c", p=128)
                dst = ring_t[:, s, 0:nk * ncol].rearrange("p (k c) -> p k c", c=ncol)
                S.dma('pool', f'slot{s}', [], [rreg], lambda e: e.dma_start(out=dst, in_=src))
                if NT > 1 and i < spt and j < state['wch']:
                    S.dma('sp', 'wcp', [rreg], [('wscr', j, j + 1)],
                          lambda e: e.dma_start(out=state['wscr'][j][:, 0:nk * ncol], in_=ring_t[:, s, 0:nk * ncol]))
            else:
                S.dma('pool', f'slot{s}', [('wscr', j, j + 1)], [rreg],
                      lambda e: e.dma_start(out=ring_t[:, s, 0:nk * ncol], in_=state['wscr'][j][:, 0:nk * ncol]))

        def next_slab(W2d, k0, nk, c0, ncol):
            assert nk * ncol <= SLOT_ELEMS
            if S.dry:
                plan.append((W2d, k0, nk, c0, ncol))
                return None, None, None
            i = state['next_consume']
            assert plan[i][1:] == (k0, nk, c0, ncol), (i, plan[i][1:], (k0, nk, c0, ncol))
            state['next_consume'] += 1
            while state['next_issue'] < min(len(plan), i + 1):
                issue_slab(state['next_issue'])
                state['next_issue'] += 1
            s = i % NSLOT
            view = ring_t[:, s, 0:nk * ncol].rearrange("p (k c) -> p k c", c=ncol)
            return view, ('ring', s * SLOT_ELEMS * 2, (s + 1) * SLOT_ELEMS * 2), i

        def after_consume(i):
            if S.dry:
                return
            while state['next_issue'] < min(len(plan), i + NSLOT + 1):
                j = state['next_issue']
                if j - NSLOT > i:
                    break
                issue_slab(j)
                state['next_issue'] += 1

        def take_half(nb):
            h = state['half']
            state['half'] ^= 1
            return [h * 4 + b for b in range(nb)]

        def proj_fm(W2d, nkc, kgroups, colbase, n_mpairs, act_fn, pieces, evac_fn, before_evac=None):
            kper = nkc // kgroups
            for mp in range(n_mpairs):
                banks = take_half(2 * len(pieces))
                for kg in range(kgroups):
                    slab, sreg, si = next_slab(W2d, kg * kper, kper, colbase + mp * 256, 256)
                    if S.dry:
                        continue
                    writes = [r_ps(b) for b in banks]
                    if before_evac is not None and mp == 0:
                        for kk in range(kper):
                            def emit_k(e, slab=slab, kg=kg, banks=banks, kk=kk):
                                inst = None
                                k = kg * kper + kk
                                for m in range(2):
                                    for pi, (c0, n) in enumerate(pieces):
                                        b = banks[m * len(pieces) + pi]
                                        inst = e.matmul(psb[b][:, 0:n], lhsT=slab[:, kk, m * 128:(m + 1) * 128],
                                                        rhs=act_fn(k, c0, n)[0],
                                                        start=(k == 0), stop=(k == nkc - 1))
                                return inst
                            S.op('pe', [sreg] + [act_fn(kg * kper + kk, c0, n)[1] for (c0, n) in pieces], writes, emit_k)
                        after_consume(si)
                        continue
                    reads = [sreg]
                    for kk in range(kper):
                        for (c0, n) in pieces:
                            reads.append(act_fn(kg * kper + kk, c0, n)[1])

                    def emit(e, slab=slab, kg=kg, banks=banks):
                        inst = None
                        for m in range(2):
                            for kk in range(kper):
                                k = kg * kper + kk
                                for pi, (c0, n) in enumerate(pieces):
                                    b = banks[m * len(pieces) + pi]
                                    inst = e.matmul(psb[b][:, 0:n], lhsT=slab[:, kk, m * 128:(m + 1) * 128],
                                                    rhs=act_fn(k, c0, n)[0],
                                                    start=(k == 0), stop=(k == nkc - 1))
                        return inst
                    S.op('pe', reads, writes, emit)
                    after_consume(si)
                if S.dry:
                    continue
                if before_evac is not None and mp == 0:
                    before_evac()
                for m in range(2):
                    for pi, (c0, n) in enumerate(pieces):
                        b = banks[m * len(pieces) + pi]
                        evac_fn(mp * 2 + m, pi, c0, n, psb[b][:, 0:n], r_ps(b))

        def sq_view(where, k, c0, n):
            if where == 'U':
                return U_bf(k, c0, n)
            if where == 'U2':
                return U_bf(28 + k, c0, n)
            return xn_t[:, k * TC + c0: k * TC + c0 + n], r_xn(k, c0, n)

        def stats_rstd(where, pieces, cols, out_row, epsv_row=None, keep_sqrt_row=None):
            c_lo, c_n = cols
            banks = take_half(len(pieces))
            for k in range(KC):
                def emit_k(e, k=k):
                    inst = None
                    for pi, (c0, n) in enumerate(pieces):
                        inst = e.matmul(psb[banks[pi]][:, 0:n], lhsT=ones_b[:, :], rhs=sq_view(where, k, c0, n)[0],
                                        start=(k == 0), stop=(k == KC - 1))
                    return inst
                S.op('pe', [R_CONST] + [sq_view(where, k, c0, n)[1] for (c0, n) in pieces], [r_ps(b) for b in banks], emit_k)
            srow = out_row if keep_sqrt_row is None else keep_sqrt_row
            for pi, (c0, n) in enumerate(pieces):
                if epsv_row is None:
                    S.op('act', [r_ps(banks[pi]), r_tiny(0)], [r_st(srow)],
                         lambda e, pi=pi, c0=c0, n=n: e.activation(out=st_t[:, srow, c0:c0 + n], in_=psb[banks[pi]][:, 0:n],
                                                                    func=AF.Sqrt, bias=tiny_t[:, 0:1], scale=1.0 / D))
                else:
                    S.op('dve', [r_ps(banks[pi]), r_st(epsv_row)], [r_st(srow)],
                         lambda e, pi=pi, c0=c0, n=n: e.scalar_tensor_tensor(
                             out=st_t[:, srow, c0:c0 + n], in0=psb[banks[pi]][:, 0:n], scalar=1.0 / D,
                             in1=st_t[:, epsv_row, c0:c0 + n], op0=ALU.mult, op1=ALU.add))
                    S.op('act', [r_st(srow)], [r_st(srow)],
                         lambda e, c0=c0, n=n: e.activation(out=st_t[:, srow, c0:c0 + n], in_=st_t[:, srow, c0:c0 + n],
                                                            func=AF.Sqrt))
            S.op('dve', [r_st(srow)], [r_st(out_row)],
                 lambda e: e.reciprocal(out=st_t[:, out_row, c_lo:c_lo + c_n], in_=st_t[:, srow, c_lo:c_lo + c_n]))

        def prenorm(gv, cols, pieces):
            c_lo, c_n = cols
            for k in range(KC):
                o, ro = sq_view('U', k, c_lo, c_n)
                S.op('act', [r_x(k, c_lo, c_n)], [ro],
                     lambda e, k=k, o=o: e.activation(out=o, in_=x_t[:, k, c_lo:c_lo + c_n], func=AF.Square))
            stats_rstd('U', pieces, cols, 0)
            for k in range(KC):
                eng = 'dve'
                S.op(eng, [r_x(k, c_lo, c_n), r_st(0), R_VECS], [r_xn(k, c_lo, c_n)],
                     lambda e, k=k: e.scalar_tensor_tensor(out=xn_t[:, k * TC + c_lo: k * TC + c_lo + c_n],
                                                           in0=x_t[:, k, c_lo:c_lo + c_n], scalar=vec(gv, k),
                                                           in1=st_t[:, 0, c_lo:c_lo + c_n], op0=ALU.mult, op1=ALU.mult))

        def prenorm_deferred(gv, cols, pieces, sq_where, want_epsv=False, want_maskr=False):
            c_lo, c_n = cols
            for k in range(KC):
                S.op('act', [r_x(k, c_lo, c_n), R_VECS], [r_xn(k, c_lo, c_n)],
                     lambda e, k=k: e.mul(out=xn_t[:, k * TC + c_lo: k * TC + c_lo + c_n], in_=x_t[:, k, c_lo:c_lo + c_n],
                                          mul=vec(gv, k)))
                o, ro = sq_view(sq_where, k, c_lo, c_n)
                S.op('act', [r_x(k, c_lo, c_n)], [ro],
                     lambda e, k=k, o=o: e.activation(out=o, in_=x_t[:, k, c_lo:c_lo + c_n], func=AF.Square))

            def emit_stats():
                if want_epsv:
                    stats_rstd(sq_where, pieces, cols, 0, keep_sqrt_row=3)
                    S.op('dve', [r_st(3)], [r_st(3)],
                         lambda e: e.scalar_tensor_tensor(out=st_t[:, 3, c_lo:c_lo + c_n], in0=st_t[:, 3, c_lo:c_lo + c_n],
                                                          scalar=EPS, in1=st_t[:, 3, c_lo:c_lo + c_n],
                                                          op0=ALU.mult, op1=ALU.mult))
                else:
                    stats_rstd(sq_where, pieces, cols, 0)
                state['half'] ^= 1
                if want_maskr:
                    S.op('dve', [r_st(0), R_MASK], [r_st(2)],
                         lambda e: e.tensor_tensor(out=st_t[:, 2, c_lo:c_lo + c_n], in0=st_t[:, 0, c_lo:c_lo + c_n],
                                                   in1=maskp_t[:, PADW + c_lo:PADW + c_lo + c_n], op=ALU.mult))
                if want_maskr == 2:
                    S.op('dve', [r_st(0), r_st(2)], [r_st(2)],
                         lambda e: e.tensor_tensor(out=st_t[:, 2, c_lo:c_lo + c_n], in0=st_t[:, 2, c_lo:c_lo + c_n],
                                                   in1=st_t[:, 0, c_lo:c_lo + c_n], op=ALU.mult))
            return emit_stats

        def make_evac_post(gv, where):
            def evac(m, pi, c0, n, ps, rps):
                o, ro = sq_view(where, m, c0, n)
                S.op('act', [rps], [ro], lambda e: e.activation(out=o, in_=ps, func=AF.Square))
                S.op('act', [rps, R_VECS], [r_ht(m, PADW + c0, n)],
                     lambda e: e.mul(out=ht_t[:, m, PADW + c0:PADW + c0 + n], in_=ps, mul=vec(gv, m)))
            return evac

        def postnorm_residual(where, cols, pieces, final=False, epsv_row=None):
            c_lo, c_n = cols
            e0 = PADW + c_lo
            stats_rstd(where, pieces, cols, 1, epsv_row=epsv_row)
            for k in range(KC):
                S.op('dve', [r_ht(k, e0, c_n), r_st(1)], [r_ht(k, e0, c_n)],
                     lambda e, k=k: e.tensor_tensor(out=ht_t[:, k, e0:e0 + c_n], in0=ht_t[:, k, e0:e0 + c_n],
                                                    in1=st_t[:, 1, c_lo:c_lo + c_n], op=ALU.mult))
                if final:
                    S.op('dve', [r_ht(k, e0, c_n), r_x(k, c_lo, c_n)], [r_ht(k, e0, c_n)],
                         lambda e, k=k: e.tensor_tensor(out=ht_t[:, k, e0:e0 + c_n], in0=x_t[:, k, c_lo:c_lo + c_n],
                                                        in1=ht_t[:, k, e0:e0 + c_n], op=ALU.add))
                else:
                    S.op('dve', [r_ht(k, e0, c_n), r_x(k, c_lo, c_n)], [r_x(k, c_lo, c_n)],
                         lambda e, k=k: e.tensor_tensor(out=x_t[:, k, c_lo:c_lo + c_n], in0=x_t[:, k, c_lo:c_lo + c_n],
                                                        in1=ht_t[:, k, e0:e0 + c_n], op=ALU.add))

        def xn_act(k, c0, n):
            return xn_t[:, k * TC + c0: k * TC + c0 + n], r_xn(k, c0, n)

        def window_sum(src_ext, src_reg, w):
            half = w // 2
            cur, cur_r, cur_len = src_ext, src_reg, HTW
            step = 1
            row = 0
            while step < w:
                new_len = cur_len - step
                dst = scr_t[:, row, 0:new_len]
                dreg = r_scr(row, 0, new_len)
                S.op('dve', [cur_r], [dreg],
                     lambda e, cur=cur, dst=dst, step=step, new_len=new_len:
                     e.tensor_tensor(out=dst, in0=cur[:, 0:new_len], in1=cur[:, step:step + new_len], op=ALU.add))
                cur, cur_r, cur_len = dst, dreg, new_len
                step *= 2
                row ^= 1
            off = PADW - half
            return cur[:, off:off + TC], cur_r

        def ffn(l, cols, pieces, final=False):
            c_lo, c_n = cols
            emit_stats = prenorm_deferred(V_FFN_PRE + l, cols, pieces, 'U2', want_epsv=True)
            for fp in range(FC // 2):
                buf = fp % 2

                def evac_gate(m, pi, c0, n, ps, rps, buf=buf):
                    mm = m % 2
                    S.op('dve', [rps, r_st(0)], [r_sgt(buf)],
                         lambda e: e.tensor_tensor(out=sgt_t[:, buf, mm, c0:c0 + n], in0=ps, in1=st_t[:, 0, c0:c0 + n],
                                                   op=ALU.mult))
                    S.op('act', [r_sgt(buf)], [r_sgt(buf)],
                         lambda e: e.activation(out=sgt_t[:, buf, mm, c0:c0 + n], in_=sgt_t[:, buf, mm, c0:c0 + n],
                                                func=AF.Silu))
                proj_fm_one(w_gate[l], fp, xn_act, pieces, evac_gate, before_evac=(emit_stats if fp == 0 else None))

                def evac_up(m, pi, c0, n, ps, rps, buf=buf, fp=fp):
                    mm = m % 2
                    o, ro = U_bf(fp * 2 + mm, c0, n)
                    S.op('dve', [rps, r_sgt(buf)], [ro],
                         lambda e: e.tensor_tensor(out=o, in0=ps, in1=sgt_t[:, buf, mm, c0:c0 + n], op=ALU.mult))
                proj_fm_one(w_up[l], fp, xn_act, pieces, evac_up)
            proj_fm(w_down[l], FC, 4, 0, KC // 2, lambda k, c0, n: U_bf(k, c0, n), pieces,
                    make_evac_post(V_FFN_POST + l, 'xn'))
            postnorm_residual('xn', cols, pieces, final=final, epsv_row=3)

        def proj_fm_one(W2d, mp, act_fn, pieces, evac_fn, before_evac=None):
            banks = take_half(2 * len(pieces))
            slab, sreg, si = next_slab(W2d, 0, KC, mp * 256, 256)
            if S.dry:
                return
            if before_evac is not None:
                for k in range(KC):
                    def emit_k(e, k=k):
                        inst = None
                        for m in range(2):
                            for pi, (c0, n) in enumerate(pieces):
                                b = banks[m * len(pieces) + pi]
                                inst = e.matmul(psb[b][:, 0:n], lhsT=slab[:, k, m * 128:(m + 1) * 128],
                                                rhs=act_fn(k, c0, n)[0], start=(k == 0), stop=(k == KC - 1))
                        return inst
                    S.op('pe', [sreg] + [act_fn(k, c0, n)[1] for (c0, n) in pieces], [r_ps(b) for b in banks], emit_k)
            else:
                reads = [sreg] + [act_fn(k, c0, n)[1] for k in range(KC) for (c0, n) in pieces]

                def emit(e):
                    inst = None
                    for m in range(2):
                        for k in range(KC):
                            for pi, (c0, n) in enumerate(pieces):
                                b = banks[m * len(pieces) + pi]
                                inst = e.matmul(psb[b][:, 0:n], lhsT=slab[:, k, m * 128:(m + 1) * 128],
                                                rhs=act_fn(k, c0, n)[0], start=(k == 0), stop=(k == KC - 1))
                    return inst
                S.op('pe', reads, [r_ps(b) for b in banks], emit)
            after_consume(si)
            if before_evac is not None:
                before_evac()
            for m in range(2):
                for pi, (c0, n) in enumerate(pieces):
                    b = banks[m * len(pieces) + pi]
                    evac_fn(mp * 2 + m, pi, c0, n, psb[b][:, 0:n], r_ps(b))

        FULL = (0, TC)
        PIECES_FULL = [(0, 280), (280, 280)]
        COREC = (HL, CORE)
        PIECES_CORE = [(HL, CORE)]

        def mixer_pool():
            cols, pieces = FULL, PIECES_FULL
            emit_stats = prenorm_deferred(V_MIX_PRE + 0, cols, pieces, 'U', want_maskr=True)
            RC_BASE = 32 * 1120
            for gi, w in enumerate(POOL_WINDOWS):
                sw, sr = window_sum(maskp_t[:, :], R_MASK, w)
                o, ro = U_f32(gi, 0, TC, base=RC_BASE)
                S.op('dve', [sr], [ro], lambda e, sw=sw, o=o: e.tensor_scalar(out=o, in0=sw, scalar1=1.0, scalar2=None, op0=ALU.max))
                S.op('dve', [ro], [ro], lambda e, o=o: e.reciprocal(out=o, in_=o))

            def evac_in(m, pi, c0, n, ps, rps):
                S.op('dve', [rps, r_st(2)], [r_ht(m, PADW + c0, n)],
                     lambda e: e.tensor_tensor(out=ht_t[:, m, PADW + c0:PADW + c0 + n], in0=ps,
                                               in1=st_t[:, 2, c0:c0 + n], op=ALU.mult))
            proj_fm(pool_w_in, KC, 1, 0, KC // 2, xn_act, pieces, evac_in, before_evac=emit_stats)
            for k in range(KC):
                gi = k // 4
                w = POOL_WINDOWS[gi]
                sw, sr = window_sum(ht_t[:, k, :], r_ht(k, 0, HTW), w)
                rc, rcr = U_f32(gi, 0, TC, base=RC_BASE)
                S.op('dve', [sr, rcr], [r_scr(2, 0, TC)],
                     lambda e, sw=sw, rc=rc: e.tensor_tensor(out=scr_t[:, 2, 0:TC], in0=sw, in1=rc, op=ALU.mult))
                o, ro = U_bf(k, 0, TC)
                S.op('dve', [r_scr(2, 0, TC), r_ht(k, PADW, TC)], [ro],
                     lambda e, o=o, k=k: e.tensor_tensor(out=o, in0=scr_t[:, 2, 0:TC], in1=ht_t[:, k, PADW:PADW + TC],
                                                         op=ALU.subtract))
            for g in range(4):
                for mp in range(2):
                    banks = take_half(4)
                    slab, sreg, si = next_slab(pool_w_grp[g], 0, 4, mp * 256, 256)
                    if S.dry:
                        continue
                    reads = [sreg] + [U_bf(g * 4 + kk, c0, n)[1] for kk in range(4) for (c0, n) in pieces]

                    def emit(e, slab=slab, banks=banks, g=g):
                        inst = None
                        for m in range(2):
                            for kk in range(4):
                                for pi, (c0, n) in enumerate(pieces):
                                    inst = e.matmul(psb[banks[m * 2 + pi]][:, 0:n], lhsT=slab[:, kk, m * 128:(m + 1) * 128],
                                                    rhs=U_bf(g * 4 + kk, c0, n)[0], start=(kk == 0), stop=(kk == 3))
                        return inst
                    S.op('pe', reads, [r_ps(b) for b in banks], emit)
                    after_consume(si)
                    for m in range(2):
                        ch = g * 4 + mp * 2 + m
                        for pi, (c0, n) in enumerate(pieces):
                            b = banks[m * 2 + pi]
                            o, ro = U_bf(16 + ch, c0, n)
                            S.op('act', [r_ps(b), R_VECS], [ro],
                                 lambda e, o=o, b=b, n=n, ch=ch: e.mul(out=o, in_=psb[b][:, 0:n], mul=vec(V_POOL_SCALE, ch)))
            proj_fm(pool_w_out, KC, 1, 0, KC // 2, lambda k, c0, n: U_bf(16 + k, c0, n), pieces,
                    make_evac_post(V_MIX_POST + 0, 'xn'))
            postnorm_residual('xn', cols, pieces)

        def mixer_sc():
            cols, pieces = FULL, PIECES_FULL
            emit_stats = prenorm_deferred(V_MIX_PRE + 1, cols, pieces, 'U2', want_epsv=True, want_maskr=2)
            for jp in range(KC // 2):
                buf = jp % 2

                def evac_gc(m, pi, c0, n, ps, rps, buf=buf):
                    mm = m % 2
                    S.op('act', [rps], [r_sgt(buf)],
                         lambda e: e.copy(out=sgt_t[:, buf, mm, c0:c0 + n], in_=ps))
                proj_fm_one(sc_w_in[:, D:2 * D], jp, xn_act, pieces, evac_gc,
                            before_evac=(emit_stats if jp == 0 else None))

                def evac_h(m, pi, c0, n, ps, rps, buf=buf):
                    mm = m % 2
                    S.op('dve', [rps, r_sgt(buf)], [r_scr(mm, 1 + c0, n)],
                         lambda e: e.tensor_tensor(out=scr_t[:, mm, 1 + c0:1 + c0 + n], in0=ps,
                                                   in1=sgt_t[:, buf, mm, c0:c0 + n], op=ALU.mult))
                proj_fm_one(sc_w_in[:, 2 * D:3 * D], jp, xn_act, pieces, evac_h)
                if not S.dry:
                    for mm in range(2):
                        ch = jp * 2 + mm
                        S.op('dve', [], [r_scr(mm, 0, 1)], lambda e, mm=mm: e.memset(scr_t[:, mm, 0:1], 0.0))
                        S.op('dve', [], [r_scr(mm, TC + 1, 1)], lambda e, mm=mm: e.memset(scr_t[:, mm, TC + 1:TC + 2], 0.0))
                        S.op('dve', [r_scr(mm, 1, TC), r_st(2)], [r_scr(mm, 1, TC)],
                             lambda e, mm=mm: e.tensor_tensor(out=scr_t[:, mm, 1:1 + TC], in0=scr_t[:, mm, 1:1 + TC],
                                                              in1=st_t[:, 2, 0:TC], op=ALU.mult))
                        S.op('dve', [r_scr(mm), R_VECS], [r_scr(2 + mm, 0, TC)],
                             lambda e, mm=mm, ch=ch: e.tensor_scalar(out=scr_t[:, 2 + mm, 0:TC], in0=scr_t[:, mm, 1:1 + TC],
                                                                     scalar1=vec(V_SC_CONV + 1, ch), scalar2=None, op0=ALU.mult))
                        for j in (0, 2):
                            S.op('dve', [r_scr(mm), r_scr(2 + mm, 0, TC), R_VECS], [r_scr(2 + mm, 0, TC)],
                                 lambda e, mm=mm, ch=ch, j=j: e.scalar_tensor_tensor(
                                     out=scr_t[:, 2 + mm, 0:TC], in0=scr_t[:, mm, j:j + TC], scalar=vec(V_SC_CONV + j, ch),
                                     in1=scr_t[:, 2 + mm, 0:TC], op0=ALU.mult, op1=ALU.add))

                def evac_gb(m, pi, c0, n, ps, rps):
                    mm = m % 2
                    o, ro = U_bf(m, c0, n)
                    S.op('dve', [rps, r_scr(2 + mm, c0, n)], [ro],
                         lambda e: e.tensor_tensor(out=o, in0=ps, in1=scr_t[:, 2 + mm, c0:c0 + n], op=ALU.mult))
                proj_fm_one(sc_w_in[:, 0:D], jp, xn_act, pieces, evac_gb)
            proj_fm(sc_w_out, KC, 1, 0, KC // 2, lambda k, c0, n: U_bf(k, c0, n), pieces,
                    make_evac_post(V_MIX_POST + 1, 'xn'))
            postnorm_residual('xn', cols, pieces, epsv_row=3)

        def mixer_cf():
            cols, pieces = FULL, PIECES_FULL
            emit_stats = prenorm_deferred(V_MIX_PRE + 2, cols, pieces, 'U2', want_maskr=True)
            CIW = 592
            HB0 = 2816
            SQ0 = HB0 + KC * CORE

            def ci_bf(par, mm, c0, n):
                off = (par * 2 + mm) * CIW + c0
                return U_t[:, off:off + n], r_U(off * 2, (off + n) * 2)

            def hb_v(k):
                off = HB0 + k * CORE
                return U_t[:, off:off + CORE], r_U(off * 2, (off + CORE) * 2)

            def sq_v(k):
                off = SQ0 + k * CORE
                return U_t[:, off:off + CORE], r_U(off * 2, (off + CORE) * 2)
            e0 = PADW + HL

            def conv_pe(jp):
                par = jp % 2
                banks = take_half(2)
                for mm in range(2):
                    ch = jp * 2 + mm
                    slab, sreg, si = next_slab(dgw[ch], 0, 1, 0, 31 * 128)
                    if S.dry:
                        continue
                    b = banks[mm]

                    def emit(e, slab=slab, b=b, mm=mm):
                        inst = None
                        for j in range(31):
                            inst = e.matmul(psb[b][:, 0:CORE], lhsT=slab[:, 0, j * 128:(j + 1) * 128],
                                            rhs=ci_bf(par, mm, HL + j, CORE)[0], start=(j == 0), stop=(j == 30))
                        return inst
                    S.op('pe', [sreg, ci_bf(par, mm, 0, CIW)[1]], [r_ps(b)], emit)
                    after_consume(si)
                    S.op('act', [r_ps(b), R_VECS], [r_ht(ch, e0, CORE)],
                         lambda e, b=b, ch=ch: e.activation(out=ht_t[:, ch, e0:e0 + CORE], in_=psb[b][:, 0:CORE],
                                                            func=AF.Identity, bias=vec(V_CF_DWB, ch)))
                    o1, r1 = hb_v(ch)
                    S.op('act', [r_ps(b), R_VECS], [r1],
                         lambda e, b=b, ch=ch, o1=o1: e.activation(out=o1, in_=psb[b][:, 0:CORE], func=AF.Identity,
                                                                   bias=vec(V_CF_DWB, ch)))
                    o2, r2 = sq_v(ch)
                    S.op('act', [r_ps(b), R_VECS], [r2],
                         lambda e, b=b, ch=ch, o2=o2: e.activation(out=o2, in_=psb[b][:, 0:CORE], func=AF.Square,
                                                                   bias=vec(V_CF_DWB, ch)))

            for jp in range(KC // 2):
                buf = jp % 2
                par = jp % 2

                def evac_gate(m, pi, c0, n, ps, rps, buf=buf):
                    mm = m % 2
                    S.op('dve', [rps, r_st(0)], [r_sgt(buf)],
                         lambda e: e.tensor_tensor(out=sgt_t[:, buf, mm, c0:c0 + n], in0=ps, in1=st_t[:, 0, c0:c0 + n],
                                                   op=ALU.mult))
                    S.op('act', [r_sgt(buf)], [r_sgt(buf)],
                         lambda e: e.activation(out=sgt_t[:, buf, mm, c0:c0 + n], in_=sgt_t[:, buf, mm, c0:c0 + n],
                                                func=AF.Sigmoid))
                proj_fm_one(cf_w_in[:, D:2 * D], jp, xn_act, pieces, evac_gate,
                            before_evac=(emit_stats if jp == 0 else None))

                def evac_a(m, pi, c0, n, ps, rps, buf=buf, par=par):
                    mm = m % 2
                    o, ro = ci_bf(par, mm, 15 + c0, n)
                    S.op('dve', [rps, r_sgt(buf)], [ro],
                         lambda e: e.tensor_tensor(out=o, in0=ps, in1=sgt_t[:, buf, mm, c0:c0 + n], op=ALU.mult))
                proj_fm_one(cf_w_in[:, 0:D], jp, xn_act, pieces, evac_a)
                if not S.dry:
                    for mm in range(2):
                        cm, cmr = ci_bf(par, mm, 15, TC)
                        S.op('dve', [cmr, r_st(2)], [cmr],
                             lambda e, cm=cm: e.tensor_tensor(out=cm, in0=cm, in1=st_t[:, 2, 0:TC], op=ALU.mult))
                if jp >= 1:
                    conv_pe(jp - 1)
            conv_pe(KC // 2 - 1)
            if S.dry:
                proj_fm(cf_w_out, KC, 1, 0, KC // 2, None, PIECES_CORE, None)
                return
            cols, pieces = COREC, PIECES_CORE
            bks = take_half(2)

            def emit_s(e):
                inst = None
                for which, fn in enumerate((hb_v, sq_v)):
                    for k in range(KC):
                        inst = e.matmul(psb[bks[which]][:, 0:CORE], lhsT=ones_b[:, :], rhs=fn(k)[0],
                                        start=(k == 0), stop=(k == KC - 1))
                return inst
            S.op('pe', [R_CONST, r_U(HB0 * 2, (SQ0 + KC * CORE) * 2)], [r_ps(b) for b in bks], emit_s)
            sl = slice(HL, HL + CORE)
            S.op('act', [r_ps(bks[0])], [r_st(2)],
                 lambda e: e.mul(out=st_t[:, 2, sl], in_=psb[bks[0]][:, 0:CORE], mul=1.0 / D))
            S.op('act', [r_ps(bks[1])], [r_st(1)],
                 lambda e: e.mul(out=st_t[:, 1, sl], in_=psb[bks[1]][:, 0:CORE], mul=1.0 / D))
            S.op('dve', [r_st(2)], [r_st(0)],
                 lambda e: e.tensor_tensor(out=st_t[:, 0, sl], in0=st_t[:, 2, sl], in1=st_t[:, 2, sl], op=ALU.mult))
            S.op('dve', [r_st(1), r_st(0)], [r_st(1)],
                 lambda e: e.tensor_tensor(out=st_t[:, 1, sl], in0=st_t[:, 1, sl], in1=st_t[:, 0, sl], op=ALU.subtract))
            S.op('dve', [r_st(1)], [r_st(1)],
                 lambda e: e.tensor_scalar(out=st_t[:, 1, sl], in0=st_t[:, 1, sl], scalar1=0.0, scalar2=EPS,
                                           op0=ALU.max, op1=ALU.add))
            S.op('act', [r_st(1)], [r_st(1)], lambda e: e.activation(out=st_t[:, 1, sl], in_=st_t[:, 1, sl], func=AF.Sqrt))
            S.op('dve', [r_st(1)], [r_st(1)], lambda e: e.reciprocal(out=st_t[:, 1, sl], in_=st_t[:, 1, sl]))
            for k in range(KC):
                S.op('dve', [r_ht(k, e0, CORE), r_st(2)], [r_ht(k, e0, CORE)],
                     lambda e, k=k: e.tensor_tensor(out=ht_t[:, k, e0:e0 + CORE], in0=ht_t[:, k, e0:e0 + CORE],
                                                    in1=st_t[:, 2, sl], op=ALU.subtract))
                S.op('dve', [r_ht(k, e0, CORE), r_st(1)], [r_ht(k, e0, CORE)],
                     lambda e, k=k: e.tensor_tensor(out=ht_t[:, k, e0:e0 + CORE], in0=ht_t[:, k, e0:e0 + CORE],
                                                    in1=st_t[:, 1, sl], op=ALU.mult))
                o, ro = U_bf(k, HL, CORE)
                S.op('act', [r_ht(k, e0, CORE), R_VECS], [ro],
                     lambda e, k=k, o=o: e.activation(out=o, in_=ht_t[:, k, e0:e0 + CORE], func=AF.Silu,
                                                      bias=vec(V_CF_LNB, k), scale=vec(V_CF_LNG, k)))
            proj_fm(cf_w_out, KC, 1, 0, KC // 2, lambda k, c0, n: U_bf(k, c0, n), pieces,
                    make_evac_post(V_MIX_POST + 2, 'xn'))
            postnorm_residual('xn', cols, pieces)

        def mixer_sg():
            cols, pieces = COREC, PIECES_CORE
            emit_stats_fm = prenorm_deferred(V_MIX_PRE + 3, cols, pieces, 'U2')

            def emit_stats():
                emit_stats_fm()
                bk = take_half(1)
                state['half'] ^= 1
                reads = [R_CONST] + [sq_view('U2', k, HL, CORE)[1] for k in range(KC)]

                def emit_c(e):
                    inst = None
                    for tc_ in range(4):
                        for k in range(KC):
                            inst = e.matmul(psb[bk[0]][:, tc_:tc_ + 1], lhsT=sq_view('U2', k, HL + tc_ * 128, 128)[0],
                                            rhs=ones_b[:, 0:1], start=(k == 0), stop=(k == KC - 1))
                    return inst
                S.op('pe', reads, [r_ps(bk[0])], emit_c)
                S.op('act', [r_ps(bk[0]), r_tiny(0)], [r_tiny(40, 4)],
                     lambda e: e.activation(out=tiny_t[:, 40:44], in_=psb[bk[0]][:, 0:4], func=AF.Sqrt,
                                            bias=tiny_t[:, 0:1], scale=1.0 / D))
                S.op('dve', [r_tiny(40, 4)], [r_tiny(40, 4)],
                     lambda e: e.reciprocal(out=tiny_t[:, 40:44], in_=tiny_t[:, 40:44]))

            def evac_u(m, pi, c0, n, ps, rps):
                S.op('dve', [rps, r_st(0)], [r_ht(m, PADW + c0, n)],
                     lambda e: e.tensor_tensor(out=ht_t[:, m, PADW + c0:PADW + c0 + n], in0=ps, in1=st_t[:, 0, c0:c0 + n],
                                               op=ALU.mult))
                S.op('act', [r_ht(m, PADW + c0, n)], [r_ht(m, PADW + c0, n)],
                     lambda e: e.activation(out=ht_t[:, m, PADW + c0:PADW + c0 + n], in_=ht_t[:, m, PADW + c0:PADW + c0 + n],
                                            func=AF.Gelu))

            def vtok(tc_, f0, n):
                b0 = (tc_ * D + f0) * 4
                return U_t[:, b0 // 2: b0 // 2 + 2 * n].bitcast(F32), r_U(b0, b0 + 4 * n)

            def vln(tc_, f0, n):
                b0 = 32768 + (tc_ * D + f0) * 2
                return U_t[:, b0 // 2: b0 // 2 + n], r_U(b0, b0 + 2 * n)
            for jp in range(KC // 2):
                banks = take_half(4)
                slab, sreg, si = next_slab(sg_w_in, 0, KC, D + jp * 256, 256)
                if S.dry:
                    continue
                if jp == 0:
                    for k in range(KC):
                        def emit_k(e, slab=slab, banks=banks, k=k):
                            inst = None
                            for tc_ in range(4):
                                c = k * TC + HL + tc_ * 128
                                inst = e.matmul(psb[banks[tc_]][:, 0:256], lhsT=xn_t[:, c:c + 128], rhs=slab[:, k, 0:256],
                                                start=(k == 0), stop=(k == KC - 1))
                            return inst
                        S.op('pe', [sreg, r_xn(k, HL, CORE)], [r_ps(b) for b in banks], emit_k)
                else:
                    reads = [sreg] + [r_xn(k, HL, CORE) for k in range(KC)]

                    def emit(e, slab=slab, banks=banks):
                        inst = None
                        for tc_ in range(4):
                            for k in range(KC):
                                c = k * TC + HL + tc_ * 128
                                inst = e.matmul(psb[banks[tc_]][:, 0:256], lhsT=xn_t[:, c:c + 128], rhs=slab[:, k, 0:256],
                                                start=(k == 0), stop=(k == KC - 1))
                        return inst
                    S.op('pe', reads, [r_ps(b) for b in banks], emit)
                after_consume(si)
                if jp == 0:
                    emit_stats()
                for tc_ in range(4):
                    o, ro = vtok(tc_, jp * 256, 256)
                    S.op('act', [r_ps(banks[tc_]), r_tiny(40 + tc_)], [ro],
                         lambda e, o=o, b=banks[tc_], tc_=tc_: e.activation(out=o, in_=psb[b][:, 0:256], func=AF.Gelu,
                                                                            scale=tiny_t[:, 40 + tc_:41 + tc_]))
            proj_fm(sg_w_in, KC, 1, 0, KC // 2, xn_act, pieces, evac_u)
            if not S.dry:
                for tc_ in range(4):
                    v_ap, v_r = vtok(tc_, 0, D)
                    j_ap, j_r = vln(tc_, 0, D)
                    c = 8 + tc_ * 8
                    S.op('dve', [v_r], [r_tiny(c)], lambda e, v_ap=v_ap, c=c: e.reduce_sum(out=tiny_t[:, c:c + 1], in_=v_ap, axis=AX.X))
                    S.op('dve', [], [r_tiny(c + 1)], lambda e, c=c: e.memset(tiny_t[:, c + 1:c + 2], 0.0))
                    S.op('act', [v_r, r_tiny(c + 1)], [j_r, r_tiny(c + 1)],
                         lambda e, v_ap=v_ap, j_ap=j_ap, c=c: e.activation(out=j_ap, in_=v_ap, func=AF.Square,
                                                                           accum_out=tiny_t[:, c + 1:c + 2]))
                    S.op('dve', [r_tiny(c, 2)], [r_tiny(c, 2)],
                         lambda e, c=c: e.tensor_scalar(out=tiny_t[:, c:c + 2], in0=tiny_t[:, c:c + 2], scalar1=1.0 / D,
                                                        scalar2=None, op0=ALU.mult))
                    S.op('dve', [r_tiny(c)], [r_tiny(c + 2)],
                         lambda e, c=c: e.tensor_tensor(out=tiny_t[:, c + 2:c + 3], in0=tiny_t[:, c:c + 1],
                                                        in1=tiny_t[:, c:c + 1], op=ALU.mult))
                    S.op('dve', [r_tiny(c + 1, 2)], [r_tiny(c + 1)],
                         lambda e, c=c: e.tensor_tensor(out=tiny_t[:, c + 1:c + 2], in0=tiny_t[:, c + 1:c + 2],
                                                        in1=tiny_t[:, c + 2:c + 3], op=ALU.subtract))
                    S.op('dve', [r_tiny(c + 1)], [r_tiny(c + 1)],
                         lambda e, c=c: e.tensor_scalar(out=tiny_t[:, c + 1:c + 2], in0=tiny_t[:, c + 1:c + 2], scalar1=0.0,
                                                        scalar2=EPS, op0=ALU.max, op1=ALU.add))
                    S.op('act', [r_tiny(c + 1)], [r_tiny(c + 1)],
                         lambda e, c=c: e.activation(out=tiny_t[:, c + 1:c + 2], in_=tiny_t[:, c + 1:c + 2], func=AF.Sqrt))
                    S.op('dve', [r_tiny(c + 1)], [r_tiny(c + 1)],
                         lambda e, c=c: e.reciprocal(out=tiny_t[:, c + 1:c + 2], in_=tiny_t[:, c + 1:c + 2]))
                    S.op('dve', [v_r, r_tiny(c, 2)], [j_r],
                         lambda e, v_ap=v_ap, j_ap=j_ap, c=c: e.tensor_scalar(out=j_ap, in0=v_ap, scalar1=tiny_t[:, c:c + 1],
                                                                              scalar2=tiny_t[:, c + 1:c + 2],
                                                                              op0=ALU.subtract, op1=ALU.mult))
                for kq in range(4):
                    banks = take_half(4)
                    reads = [R_CONST] + [vln(tc_, kq * 512, 512)[1] for tc_ in range(4)]

                    def emit(e, kq=kq, banks=banks):
                        inst = None
                        for kk in range(4):
                            k = kq * 4 + kk
                            for tc_ in range(4):
                                inst = e.matmul(psb[banks[kk]][:, tc_ * 128:(tc_ + 1) * 128],
                                                lhsT=vln(tc_, k * 128, 128)[0], rhs=wsT_t[:, k // 2, :],
                                                start=True, stop=True)
                        return inst
                    S.op('pe', reads, [r_ps(b) for b in banks], emit)
                    for kk in range(4):
                        k = kq * 4 + kk
                        h = k // 2
                        b = banks[kk]
                        S.op('dve', [R_R, R_CONST, R_VECS], [r_scr(0, 0, 128)],
                             lambda e, k=k, h=h: e.scalar_tensor_tensor(out=scr_t[:, 0, 0:128], in0=R_t[:, h, :],
                                                                        scalar=vec(V_SG_LNB, k), in1=bsb_t[:, h, :],
                                                                        op0=ALU.mult, op1=ALU.add))
                        for tc_ in range(4):
                            S.op('dve', [r_ps(b), r_scr(0, 0, 128), R_VECS], [r_scr(1, tc_ * 128, 128)],
                                 lambda e, k=k, b=b, tc_=tc_: e.scalar_tensor_tensor(
                                     out=scr_t[:, 1, tc_ * 128:(tc_ + 1) * 128], in0=psb[b][:, tc_ * 128:(tc_ + 1) * 128],
                                     scalar=vec(V_SG_LNG, k), in1=scr_t[:, 0, 0:128], op0=ALU.mult, op1=ALU.add))
                        S.op('dve', [r_scr(1, 0, 512), r_ht(k, PADW + HL, CORE)], [r_xn(k, HL, CORE)],
                             lambda e, k=k: e.tensor_tensor(out=xn_t[:, k * TC + HL:k * TC + HL + CORE], in0=scr_t[:, 1, 0:512],
                                                            in1=ht_t[:, k, PADW + HL:PADW + HL + CORE], op=ALU.mult))
            proj_fm(sg_w_out, KC, 1, 0, KC // 2, xn_act, pieces, make_evac_post(V_MIX_POST + 3, 'U'))
            postnorm_residual('U', cols, pieces)

        def load_x(i):
            for k in range(KC):
                S.dma('sp', f'xl{k}', [], [r_x(k, 0, TC)],
                      lambda e, k=k: e.dma_start(out=x_t[:, k, :], in_=xin[i][k * 128:(k + 1) * 128, :]))

        def load_mask(i):
            S.dma('sp', 'mld', [], [R_MASK],
                  lambda e: e.dma_start(out=maskp_t[:, PADW:PADW + TC], in_=maskd[i]))

        def tile_body(i):
            if i == 0 and not S.dry:
                load_x(0)
                load_mask(0)
            mixer_pool()
            ffn(0, FULL, PIECES_FULL)
            mixer_sc()
            ffn(1, FULL, PIECES_FULL)
            mixer_cf()
            if i + 1 < NT and not S.dry:
                load_mask(i + 1)
            ffn(2, COREC, PIECES_CORE)
            mixer_sg()
            ffn(3, COREC, PIECES_CORE, final=True)
            if not S.dry:
                if i + 1 < NT:
                    load_x(i + 1)
                S.dma('sp', 'yst', [('ht', 0, KC * HTW * 4)], [],
                      lambda e: e.dma_start(out=yout[i].rearrange("(k p) t -> p k t", p=128),
                                            in_=ht_t[:, :, PADW + HL:PADW + HL + CORE]))

        def prologue():
            S.dma('sp', 'cst', [], [R_VECS], lambda e: e.dma_start(out=vecs_t[:, :, :].rearrange("p v k -> p (v k)"), in_=vecsd))
            S.dma('sp', 'cst', [], [R_CONST], lambda e: e.dma_start(out=bsb_t[:, :, :].rearrange("p h q -> p (h q)"), in_=bsbd))
            S.dma('pool', 'cst', [], [R_CONST], lambda e: e.dma_start(out=wsT_t[:, :, :].rearrange("p h q -> p (h q)"), in_=wsTd))
            S.dma('pool', 'cst', [], [R_CONST], lambda e: e.dma_start(out=ident_b[:, :], in_=identd))
            S.op('dve', [], [R_CONST], lambda e: e.memset(ones_f[:, :], 1.0))
            S.op('dve', [], [R_CONST], lambda e: e.memset(ones_b[:, :], 1.0))
            S.op('dve', [], [r_tiny(0)], lambda e: e.memset(tiny_t[:, 0:1], EPS))
            S.op('dve', [], [R_MASK], lambda e: e.memset(maskp_t[:, :], 0.0))
            S.op('dve', [], [('ht', 0, KC * HTW * 4)], lambda e: e.memset(ht_t[:, :, :].rearrange("p k t -> p (k t)"), 0.0))
            S.op('dve', [], [r_scr(0), r_scr(1), r_scr(2), r_scr(3)],
                 lambda e: e.memset(scr_t[:, :, :].rearrange("p k t -> p (k t)"), 0.0))
            for hb in range(2):
                banks = take_half(1)

                def emit(e, hb=hb, banks=banks):
                    inst = None
                    for hh in range(4):
                        inst = e.matmul(psb[banks[0]][:, hh * 128:(hh + 1) * 128], lhsT=ones_b[:, :],
                                        rhs=wsT_t[:, hb * 4 + hh, :], start=True, stop=True)
                    return inst
                S.op('pe', [R_CONST], [r_ps(banks[0])], emit)
                S.op('act', [r_ps(banks[0])], [R_R],
                     lambda e, hb=hb, banks=banks: e.activation(out=R_t[:, hb * 4:(hb + 1) * 4, :].rearrange("p h q -> p (h q)"),
                                                                in_=psb[banks[0]][:, 0:512], func=AF.Identity))

        S.dry = True
        for i in range(NT):
            tile_body(i)
        S.dry = False
        state['half'] = 0
        assert len(plan) % NT == 0
        state['spt'] = len(plan) // NT
        for i in range(len(plan)):
            assert plan[i][1:] == plan[i % state['spt']][1:]
        WCH = min(224, state['spt'])
        state['wch'] = WCH
        wpart = nc.dram_tensor("wscr", [WCH, 128, SLOT_ELEMS], BF16, kind="Internal").ap()
        state['wscr'] = [wpart[j] for j in range(WCH)]
        prologue()
        for i in range(NT):
            tile_body(i)
        assert state['next_consume'] == len(plan), (state['next_consume'], len(plan))
        S.final_wait('sp', ['yst'])

        with nc.Block() as block:
            @block.tensor
            def _(e):
                for f in S.prog['pe']:
                    f(e)

            @block.scalar
            def _(e):
                for f in S.prog['act']:
                    f(e)

            @block.vector
            def _(e):
                for f in S.prog['dve']:
                    f(e)

            @block.gpsimd
            def _(e):
                for f in S.prog['pool']:
                    f(e)

            @block.sync
            def _(e):
                for f in S.prog['sp']:
                    f(e)
        print("program sizes", {k: len(v) for k, v in S.prog.items()}, "slabs", len(plan), flush=True)
    return nc


def _tiles_layout(seqs, n_cores, NT):
    tiles = []
    for si, a in enumerate(seqs):
        assert a.shape[0] % CORE == 0
        for s0 in range(0, a.shape[0], CORE):
            tiles.append((si, s0))
    assert len(tiles) == n_cores * NT, (len(tiles), n_cores, NT)
    return tiles


def run_trunk_device(seqs, W, n_cores, NT):
    tiles = _tiles_layout(seqs, n_cores, NT)
    xin = np.zeros((n_cores, NT, D, TC), np.float32)
    maskb = np.zeros((n_cores, NT, 128, TC), np.float32)
    for t, (si, s0) in enumerate(tiles):
        c, i = divmod(t, NT)
        a = seqs[si]
        Sl = a.shape[0]
        lo, hi = max(0, s0 - HL), min(Sl, s0 + CORE + HL)
        o0 = lo - (s0 - HL)
        xin[c, i, :, o0:o0 + (hi - lo)] = a[lo:hi, :].T
        maskb[c, i, :, o0:o0 + (hi - lo)] = 1.0

    def pk(v):
        return np.asarray(v, np.float32).reshape(KC, 128).T
    vecs = np.zeros((128, NV, KC), np.float32)
    for l in range(4):
        vecs[:, V_MIX_PRE + l] = pk(W["mix_pre_g"][l])
        vecs[:, V_MIX_POST + l] = pk(W["mix_post_g"][l])
        vecs[:, V_FFN_PRE + l] = pk(W["ffn_pre_g"][l])
        vecs[:, V_FFN_POST + l] = pk(W["ffn_post_g"][l])
    vecs[:, V_POOL_SCALE] = pk(W["pool_scale"][0])
    vecs[:, V_CF_DWB] = pk(W["cf_dw_b"][0])
    vecs[:, V_CF_LNG] = pk(W["cf_ln_g"][0])
    vecs[:, V_CF_LNB] = pk(W["cf_ln_b"][0])
    vecs[:, V_SG_LNG] = pk(W["sg_ln_g"][0])
    vecs[:, V_SG_LNB] = pk(W["sg_ln_b"][0])
    for j in range(3):
        vecs[:, V_SC_CONV + j] = pk(W["sc_conv_w"][0, j])
    for j in range(31):
        vecs[:, V_CF_DW + j] = pk(W["cf_dw_w"][0, j])
    vecs = np.ascontiguousarray(vecs.reshape(128, NV * KC))
    dgw = np.zeros((KC, 128, 31, 128), np.float32)
    cw = np.asarray(W["cf_dw_w"][0], np.float32).reshape(31, KC, 128)
    pidx = np.arange(128)
    for j in range(31):
        dgw[:, pidx, j, pidx] = cw[j]
    dgw = dgw.reshape(KC, 128, 31 * 128)
    wsT = np.ascontiguousarray(np.asarray(W["sg_w_s"][0], np.float32).transpose(2, 0, 1).reshape(128, 8 * 128))
    bsb = np.ascontiguousarray(np.broadcast_to(np.asarray(W["sg_b_s"][0], np.float32).reshape(1, 8 * 128), (128, 8 * 128)))

    shared = {
        "vecs": vecs, "wsT": wsT, "bsb": bsb, "ident": np.eye(128, dtype=np.float32), "dgw": dgw,
        "ffn_w_gate": np.asarray(W["ffn_w_gate"], np.float32),
        "ffn_w_up": np.asarray(W["ffn_w_up"], np.float32),
        "ffn_w_down": np.asarray(W["ffn_w_down"], np.float32),
        "pool_w_in": np.asarray(W["pool_w_in"][0], np.float32),
        "pool_w_grp": np.asarray(W["pool_w_grp"][0], np.float32),
        "pool_w_out": np.asarray(W["pool_w_out"][0], np.float32),
        "sc_w_in": np.asarray(W["sc_w_in"][0], np.float32),
        "sc_w_out": np.asarray(W["sc_w_out"][0], np.float32),
        "cf_w_in": np.asarray(W["cf_w_in"][0], np.float32),
        "cf_w_out": np.asarray(W["cf_w_out"][0], np.float32),
        "sg_w_in": np.asarray(W["sg_w_in"][0], np.float32),
        "sg_w_out": np.asarray(W["sg_w_out"][0], np.float32),
    }
    nc = build_program(NT)
    in_maps = []
    for c in range(n_cores):
        m = dict(shared)
        m["xin"] = xin[c]
        m["maskb"] = maskb[c]
        in_maps.append(m)
    res = run_bass_kernel_spmd(nc, in_maps, core_ids=list(range(n_cores)))
    outs = [np.empty_like(a) for a in seqs]
    for t, (si, s0) in enumerate(tiles):
        c, i = divmod(t, NT)
        outs[si][s0:s0 + CORE, :] = res.results[c]["yout"][i].T
    return outs


def kernel(**inputs):
    xp = np.asarray(inputs["x_prompt"], np.float32)
    xs = np.asarray(inputs["x_sample"], np.float32)
    seqs = [xp[b] for b in range(xp.shape[0])] + [xs[b] for b in range(xs.shape[0])]
    outs = run_trunk_device(seqs, inputs, 8, 12)
    y_prompt = np.stack(outs[:xp.shape[0]], axis=0)
    y_sample = np.stack(outs[xp.shape[0]:], axis=0)
    return (y_prompt, y_sample)
```

```python
import numpy as np
import concourse.bass as bass
import concourse.mybir as mybir
from concourse.bass_utils import run_bass_kernel_spmd

F32 = mybir.dt.float32
BF16 = mybir.dt.bfloat16
AF = mybir.ActivationFunctionType
ALU = mybir.AluOpType
AX = mybir.AxisListType

D = 2048
KC = 16
DFF = 5632
FC = 44
HL = 24
CORE = 512
TC = CORE + 2 * HL
PADW = 8
HTW = TC + 2 * PADW
EPS = 1e-6
NSLOT = 3
SLOT_ELEMS = 4096
NV = 56
POOL_WINDOWS = (2, 4, 8, 16)
SCRW = 592

V_MIX_PRE, V_MIX_POST, V_FFN_PRE, V_FFN_POST = 0, 4, 8, 12
V_POOL_SCALE, V_CF_DWB, V_CF_LNG, V_CF_LNB, V_SG_LNG, V_SG_LNB = 16, 17, 18, 19, 20, 21
V_SC_CONV, V_CF_DW = 22, 25


class Sched:
    def __init__(self, engs, semh):
        self.E = engs
        self.semh = semh
        self.cnt = {k: 0 for k in semh}
        self.known = {e: {} for e in engs}
        self.cells = {}
        self.cellsz = {}
        self.dry = False
        self.prog = {e: [] for e in engs}

    def alloc(self, name, cell_bytes):
        self.cellsz[name] = cell_bytes

    def _cells(self, regs):
        for (a, b0, b1) in regs:
            cs = self.cellsz[a]
            for c in range(b0 // cs, (b1 - 1) // cs + 1):
                yield (a, c)

    def _gather(self, reads, writes):
        deps = {}

        def add(k, v):
            if deps.get(k, 0) < v:
                deps[k] = v
        for c in self._cells(reads):
            st = self.cells.get(c)
            if st and st[0] is not None:
                add(*st[0])
        for c in self._cells(writes):
            st = self.cells.get(c)
            if st:
                if st[0] is not None:
                    add(*st[0])
                for k, v in st[1].items():
                    add(k, v)
        return deps

    def _record(self, tok, reads, writes):
        k, v = tok
        for c in self._cells(writes):
            self.cells[c] = [tok, {}]
        for c in self._cells(reads):
            st = self.cells.setdefault(c, [None, {}])
            if st[1].get(k, 0) < v:
                st[1][k] = v

    def _waits(self, eng, deps):
        kn = self.known[eng]
        for k, v in deps.items():
            if k == eng and eng == 'pe':
                continue
            if kn.get(k, 0) >= v:
                continue
            self.prog[eng].append(lambda e, s=self.semh[k], v=v: e.wait_ge(s, v))
            kn[k] = v

    def op(self, eng, reads, writes, emit):
        if self.dry:
            return
        self._waits(eng, self._gather(reads, writes))
        self.cnt[eng] += 1
        self.prog[eng].append(lambda e, emit=emit, s=self.semh[eng]: emit(e).then_inc(s, 1))
        self._record((eng, self.cnt[eng]), reads, writes)

    def dma(self, eng, semname, reads, writes, emit):
        if self.dry:
            return
        self._waits(eng, self._gather(reads, writes))
        self.cnt[semname] += 16
        self.prog[eng].append(lambda e, emit=emit, s=self.semh[semname]: emit(e).then_inc(s, 16))
        self._record((semname, self.cnt[semname]), reads, writes)

    def final_wait(self, eng, keys):
        for k in keys:
            if self.cnt[k] > 0:
                self.prog[eng].append(lambda e, s=self.semh[k], v=self.cnt[k]: e.wait_ge(s, v))


def build_program(NT):
    nc = bass.Bass("TRN2", target_bir_lowering=False)

    def din(name, shape):
        return nc.dram_tensor(name, list(shape), F32, kind="ExternalInput").ap()

    xin = din("xin", [NT, D, TC])
    maskd = din("maskb", [NT, 128, TC])
    vecsd = din("vecs", [128, NV * KC])
    wsTd = din("wsT", [128, 8 * 128])
    bsbd = din("bsb", [128, 8 * 128])
    identd = din("ident", [128, 128])
    dgw = din("dgw", [KC, 128, 31 * 128])
    w_gate = din("ffn_w_gate", [4, D, DFF])
    w_up = din("ffn_w_up", [4, D, DFF])
    w_down = din("ffn_w_down", [4, DFF, D])
    pool_w_in = din("pool_w_in", [D, D])
    pool_w_grp = din("pool_w_grp", [4, 512, 512])
    pool_w_out = din("pool_w_out", [D, D])
    sc_w_in = din("sc_w_in", [D, 3 * D])
    sc_w_out = din("sc_w_out", [D, D])
    cf_w_in = din("cf_w_in", [D, 2 * D])
    cf_w_out = din("cf_w_out", [D, D])
    sg_w_in = din("sg_w_in", [D, 2 * D])
    sg_w_out = din("sg_w_out", [D, D])
    yout = nc.dram_tensor("yout", [NT, D, CORE], F32, kind="ExternalOutput").ap()

    from contextlib import ExitStack
    with ExitStack() as es:
        def sb(name, shape, dt):
            return es.enter_context(nc.sbuf_tensor(name, list(shape), dt))

        x_t = sb("x_t", [128, KC, TC], F32)
        xn_t = sb("xn_t", [128, KC * TC], BF16)
        ht_t = sb("ht_t", [128, KC, HTW], F32)
        U_t = sb("U_t", [128, FC * TC], BF16)
        ring_t = sb("ring_t", [128, NSLOT, SLOT_ELEMS], BF16)
        vecs_t = sb("vecs_t", [128, NV, KC], F32)
        maskp_t = sb("maskp_t", [128, HTW], F32)
        sgt_t = sb("sgt_t", [128, 2, 2, TC], F32)
        scr_t = sb("scr_t", [128, 4, SCRW], F32)
        st_t = sb("st_t", [128, 4, TC], F32)
        wsT_t = sb("wsT_t", [128, 8, 128], BF16)
        bsb_t = sb("bsb_t", [128, 8, 128], F32)
        R_t = sb("R_t", [128, 8, 128], F32)
        ones_f = sb("ones_f", [128, 128], F32)
        ones_b = sb("ones_b", [128, 128], BF16)
        ident_b = sb("ident_b", [128, 128], BF16)
        tiny_t = sb("tiny_t", [128, 64], F32)
        psb = [es.enter_context(nc.psum_tensor(f"ps{b}", [128, 512], F32)) for b in range(8)]

        semnames = ['pe', 'act', 'dve', 'pool', 'sp', 'mld', 'yst', 'cst'] + [f'xl{k}' for k in range(KC)] + [f'slot{s}' for s in range(NSLOT)]
        semh = {k: es.enter_context(nc.semaphore(k)) for k in semnames}
        engs = {k: None for k in ('pe', 'act', 'dve', 'pool', 'sp')}
        S = Sched(engs, semh)
        S.alloc('x', 1120)
        S.alloc('xn', 560)
        S.alloc('ht', HTW * 4)
        S.alloc('U', 560)
        S.alloc('ring', SLOT_ELEMS * 2)
        S.alloc('vecs', 1 << 20)
        S.alloc('mask', 1 << 20)
        S.alloc('sgt', 2 * TC * 4)
        S.alloc('scr', SCRW * 4)
        S.alloc('st', TC * 4)
        S.alloc('const', 1 << 20)
        S.alloc('R', 1 << 20)
        S.alloc('tiny', 4)
        S.alloc('ps', 2048)

        def r_x(k, c0, n): return ('x', (k * TC + c0) * 4, (k * TC + c0 + n) * 4)
        def r_xn(k, c0, n): return ('xn', (k * TC + c0) * 2, (k * TC + c0 + n) * 2)
        def r_ht(k, e0, n): return ('ht', (k * HTW + e0) * 4, (k * HTW + e0 + n) * 4)
        def r_U(b0, b1): return ('U', b0, b1)
        def r_ps(b): return ('ps', b * 2048, (b + 1) * 2048)
        def r_scr(r, c0=0, n=SCRW): return ('scr', (r * SCRW + c0) * 4, (r * SCRW + c0 + n) * 4)
        def r_st(r): return ('st', r * TC * 4, (r + 1) * TC * 4)
        def r_sgt(b): return ('sgt', b * 2 * TC * 4, (b + 1) * 2 * TC * 4)
        def r_tiny(c, n=1): return ('tiny', c * 4, (c + n) * 4)
        R_VECS = ('vecs', 0, 1)
        R_MASK = ('mask', 0, 1)
        R_CONST = ('const', 0, 1)
        R_R = ('R', 0, 1)

        def U_bf(k, c0, n):
            return U_t[:, k * TC + c0: k * TC + c0 + n], r_U((k * TC + c0) * 2, (k * TC + c0 + n) * 2)

        def U_f32(k, c0, n, base=0):
            b0 = base + (k * TC + c0) * 4
            return (U_t[:, b0 // 2: b0 // 2 + 2 * n].bitcast(F32), r_U(b0, b0 + 4 * n))

        def vec(v, k):
            return vecs_t[:, v, k:k + 1]

        plan = []
        state = {'next_consume': 0, 'next_issue': 0, 'half': 0}

        def issue_slab(i):
            W2d, k0, nk, c0, ncol = plan[i]
            s = i % NSLOT
            src = W2d[k0 * 128:(k0 + nk) * 128, c0:c0 + ncol].rearrange("(k p) c -> p k c", p=128)
            dst = ring_t[:, s, 0:nk * ncol].rearrange("p (k c) -> p k c", c=ncol)
            S.dma('pool', f'slot{s}', [], [('ring', s * SLOT_ELEMS * 2, (s + 1) * SLOT_ELEMS * 2)],
                  lambda e: e.dma_start(out=dst, in_=src))

        def next_slab(W2d, k0, nk, c0, ncol):
            assert nk * ncol <= SLOT_ELEMS
            if S.dry:
                plan.append((W2d, k0, nk, c0, ncol))
                return None, None, None
            i = state['next_consume']
            assert plan[i][1:] == (k0, nk, c0, ncol), (i, plan[i][1:], (k0, nk, c0, ncol))
            state['next_consume'] += 1
            while state['next_issue'] < min(len(plan), i + 1):
                issue_slab(state['next_issue'])
                state['next_issue'] += 1
            s = i % NSLOT
            view = ring_t[:, s, 0:nk * ncol].rearrange("p (k c) -> p k c", c=ncol)
            return view, ('ring', s * SLOT_ELEMS * 2, (s + 1) * SLOT_ELEMS * 2), i

        def after_consume(i):
            if S.dry:
                return
            while state['next_issue'] < min(len(plan), i + NSLOT + 1):
                j = state['next_issue']
                if j - NSLOT > i:
                    break
                issue_slab(j)
                state['next_issue'] += 1

        def take_half(nb):
            h = state['half']
            state['half'] ^= 1
            return [h * 4 + b for b in range(nb)]

        def proj_fm(W2d, nkc, kgroups, colbase, n_mpairs, act_fn, pieces, evac_fn, before_evac=None):
            kper = nkc // kgroups
            for mp in range(n_mpairs):
                banks = take_half(2 * len(pieces))
                for kg in range(kgroups):
                    slab, sreg, si = next_slab(W2d, kg * kper, kper, colbase + mp * 256, 256)
                    if S.dry:
                        continue
                    writes = [r_ps(b) for b in banks]
                    if before_evac is not None and mp == 0:
                        for kk in range(kper):
                            def emit_k(e, slab=slab, kg=kg, banks=banks, kk=kk):
                                inst = None
                                k = kg * kper + kk
                                for m in range(2):
                                    for pi, (c0, n) in enumerate(pieces):
                                        b = banks[m * len(pieces) + pi]
                                        inst = e.matmul(psb[b][:, 0:n], lhsT=slab[:, kk, m * 128:(m + 1) * 128],
                                                        rhs=act_fn(k, c0, n)[0],
                                                        start=(k == 0), stop=(k == nkc - 1))
                                return inst
                            S.op('pe', [sreg] + [act_fn(kg * kper + kk, c0, n)[1] for (c0, n) in pieces], writes, emit_k)
                        after_consume(si)
                        continue
                    reads = [sreg]
                    for kk in range(kper):
                        for (c0, n) in pieces:
                            reads.append(act_fn(kg * kper + kk, c0, n)[1])

                    def emit(e, slab=slab, kg=kg, banks=banks):
                        inst = None
                        for m in range(2):
                            for kk in range(kper):
                                k = kg * kper + kk
                                for pi, (c0, n) in enumerate(pieces):
                                    b = banks[m * len(pieces) + pi]
                                    inst = e.matmul(psb[b][:, 0:n], lhsT=slab[:, kk, m * 128:(m + 1) * 128],
                                                    rhs=act_fn(k, c0, n)[0],
                                                    start=(k == 0), stop=(k == nkc - 1))
                        return inst
                    S.op('pe', reads, writes, emit)
                    after_consume(si)
                if S.dry:
                    continue
                if before_evac is not None and mp == 0:
                    before_evac()
                for ph in getattr(evac_fn, 'phases', (evac_fn,)):
                    for m in range(2):
                        for pi, (c0, n) in enumerate(pieces):
                            b = banks[m * len(pieces) + pi]
                            ph(mp * 2 + m, pi, c0, n, psb[b][:, 0:n], r_ps(b))

        def sq_view(where, k, c0, n):
            if where == 'U':
                return U_bf(k, c0, n)
            if where == 'U2':
                return U_bf(28 + k, c0, n)
            return xn_t[:, k * TC + c0: k * TC + c0 + n], r_xn(k, c0, n)

        def stats_rstd(where, pieces, cols, out_row, epsv_row=None, keep_sqrt_row=None):
            c_lo, c_n = cols
            banks = take_half(len(pieces))
            for k in range(KC):
                def emit_k(e, k=k):
                    inst = None
                    for pi, (c0, n) in enumerate(pieces):
                        inst = e.matmul(psb[banks[pi]][:, 0:n], lhsT=ones_b[:, :], rhs=sq_view(where, k, c0, n)[0],
                                        start=(k == 0), stop=(k == KC - 1))
                    return inst
                S.op('pe', [R_CONST] + [sq_view(where, k, c0, n)[1] for (c0, n) in pieces], [r_ps(b) for b in banks], emit_k)
            srow = out_row if keep_sqrt_row is None else keep_sqrt_row
            for pi, (c0, n) in enumerate(pieces):
                if epsv_row is None:
                    S.op('act', [r_ps(banks[pi]), r_tiny(0)], [r_st(srow)],
                         lambda e, pi=pi, c0=c0, n=n: e.activation(out=st_t[:, srow, c0:c0 + n], in_=psb[banks[pi]][:, 0:n],
                                                                    func=AF.Sqrt, bias=tiny_t[:, 0:1], scale=1.0 / D))
                else:
                    S.op('dve', [r_ps(banks[pi]), r_st(epsv_row)], [r_st(srow)],
                         lambda e, pi=pi, c0=c0, n=n: e.scalar_tensor_tensor(
                             out=st_t[:, srow, c0:c0 + n], in0=psb[banks[pi]][:, 0:n], scalar=1.0 / D,
                             in1=st_t[:, epsv_row, c0:c0 + n], op0=ALU.mult, op1=ALU.add))
                    S.op('act', [r_st(srow)], [r_st(srow)],
                         lambda e, c0=c0, n=n: e.activation(out=st_t[:, srow, c0:c0 + n], in_=st_t[:, srow, c0:c0 + n],
                                                            func=AF.Sqrt))
            S.op('dve', [r_st(srow)], [r_st(out_row)],
                 lambda e: e.reciprocal(out=st_t[:, out_row, c_lo:c_lo + c_n], in_=st_t[:, srow, c_lo:c_lo + c_n]))

        def prenorm(gv, cols, pieces):
            c_lo, c_n = cols
            for k in range(KC):
                o, ro = sq_view('U', k, c_lo, c_n)
                S.op('act', [r_x(k, c_lo, c_n)], [ro],
                     lambda e, k=k, o=o: e.activation(out=o, in_=x_t[:, k, c_lo:c_lo + c_n], func=AF.Square))
            stats_rstd('U', pieces, cols, 0)
            for k in range(KC):
                eng = 'dve'
                S.op(eng, [r_x(k, c_lo, c_n), r_st(0), R_VECS], [r_xn(k, c_lo, c_n)],
                     lambda e, k=k: e.scalar_tensor_tensor(out=xn_t[:, k * TC + c_lo: k * TC + c_lo + c_n],
                                                           in0=x_t[:, k, c_lo:c_lo + c_n], scalar=vec(gv, k),
                                                           in1=st_t[:, 0, c_lo:c_lo + c_n], op0=ALU.mult, op1=ALU.mult))

        def prenorm_deferred(gv, cols, pieces, sq_where, want_epsv=False, want_maskr=False):
            c_lo, c_n = cols
            for k in range(KC):
                S.op('act', [r_x(k, c_lo, c_n), R_VECS], [r_xn(k, c_lo, c_n)],
                     lambda e, k=k: e.mul(out=xn_t[:, k * TC + c_lo: k * TC + c_lo + c_n], in_=x_t[:, k, c_lo:c_lo + c_n],
                                          mul=vec(gv, k)))
                o, ro = sq_view(sq_where, k, c_lo, c_n)
                S.op('act', [r_x(k, c_lo, c_n)], [ro],
                     lambda e, k=k, o=o: e.activation(out=o, in_=x_t[:, k, c_lo:c_lo + c_n], func=AF.Square))

            def emit_stats():
                if want_epsv:
                    stats_rstd(sq_where, pieces, cols, 0, keep_sqrt_row=3)
                    S.op('dve', [r_st(3)], [r_st(3)],
                         lambda e: e.scalar_tensor_tensor(out=st_t[:, 3, c_lo:c_lo + c_n], in0=st_t[:, 3, c_lo:c_lo + c_n],
                                                          scalar=EPS, in1=st_t[:, 3, c_lo:c_lo + c_n],
                                                          op0=ALU.mult, op1=ALU.mult))
                else:
                    stats_rstd(sq_where, pieces, cols, 0)
                state['half'] ^= 1
                if want_maskr:
                    S.op('dve', [r_st(0), R_MASK], [r_st(2)],
                         lambda e: e.tensor_tensor(out=st_t[:, 2, c_lo:c_lo + c_n], in0=st_t[:, 0, c_lo:c_lo + c_n],
                                                   in1=maskp_t[:, PADW + c_lo:PADW + c_lo + c_n], op=ALU.mult))
                if want_maskr == 2:
                    S.op('dve', [r_st(0), r_st(2)], [r_st(2)],
                         lambda e: e.tensor_tensor(out=st_t[:, 2, c_lo:c_lo + c_n], in0=st_t[:, 2, c_lo:c_lo + c_n],
                                                   in1=st_t[:, 0, c_lo:c_lo + c_n], op=ALU.mult))
            return emit_stats

        def make_evac_post(gv, where):
            def evac_sq(m, pi, c0, n, ps, rps):
                o, ro = sq_view(where, m, c0, n)
                S.op('act', [rps], [ro], lambda e: e.activation(out=o, in_=ps, func=AF.Square))

            def evac_mul(m, pi, c0, n, ps, rps):
                S.op('act', [rps, R_VECS], [r_ht(m, PADW + c0, n)],
                     lambda e: e.mul(out=ht_t[:, m, PADW + c0:PADW + c0 + n], in_=ps, mul=vec(gv, m)))

            def evac(m, pi, c0, n, ps, rps):
                evac_sq(m, pi, c0, n, ps, rps)
                evac_mul(m, pi, c0, n, ps, rps)
            evac.phases = (evac_sq, evac_mul)
            return evac

        def postnorm_residual(where, cols, pieces, final=False, epsv_row=None):
            c_lo, c_n = cols
            e0 = PADW + c_lo
            stats_rstd(where, pieces, cols, 1, epsv_row=epsv_row)
            for k in range(KC):
                S.op('dve', [r_ht(k, e0, c_n), r_st(1)], [r_ht(k, e0, c_n)],
                     lambda e, k=k: e.tensor_tensor(out=ht_t[:, k, e0:e0 + c_n], in0=ht_t[:, k, e0:e0 + c_n],
                                                    in1=st_t[:, 1, c_lo:c_lo + c_n], op=ALU.mult))
                if final:
                    S.op('dve', [r_ht(k, e0, c_n), r_x(k, c_lo, c_n)], [r_ht(k, e0, c_n)],
                         lambda e, k=k: e.tensor_tensor(out=ht_t[:, k, e0:e0 + c_n], in0=x_t[:, k, c_lo:c_lo + c_n],
                                                        in1=ht_t[:, k, e0:e0 + c_n], op=ALU.add))
                else:
                    S.op('dve', [r_ht(k, e0, c_n), r_x(k, c_lo, c_n)], [r_x(k, c_lo, c_n)],
                         lambda e, k=k: e.tensor_tensor(out=x_t[:, k, c_lo:c_lo + c_n], in0=x_t[:, k, c_lo:c_lo + c_n],
                                                        in1=ht_t[:, k, e0:e0 + c_n], op=ALU.add))

        def xn_act(k, c0, n):
            return xn_t[:, k * TC + c0: k * TC + c0 + n], r_xn(k, c0, n)

        def window_sum(src_ext, src_reg, w):
            half = w // 2
            cur, cur_r, cur_len = src_ext, src_reg, HTW
            step = 1
            row = 0
            while step < w:
                new_len = cur_len - step
                dst = scr_t[:, row, 0:new_len]
                dreg = r_scr(row, 0, new_len)
                S.op('dve', [cur_r], [dreg],
                     lambda e, cur=cur, dst=dst, step=step, new_len=new_len:
                     e.tensor_tensor(out=dst, in0=cur[:, 0:new_len], in1=cur[:, step:step + new_len], op=ALU.add))
                cur, cur_r, cur_len = dst, dreg, new_len
                step *= 2
                row ^= 1
            off = PADW - half
            return cur[:, off:off + TC], cur_r

        def ffn(l, cols, pieces, final=False):
            c_lo, c_n = cols
            emit_stats = prenorm_deferred(V_FFN_PRE + l, cols, pieces, 'U2', want_epsv=True)
            for fp in range(FC // 2):
                buf = fp % 2

                def evac_gate(m, pi, c0, n, ps, rps, buf=buf):
                    mm = m % 2
                    S.op('dve', [rps, r_st(0)], [r_sgt(buf)],
                         lambda e: e.tensor_tensor(out=sgt_t[:, buf, mm, c0:c0 + n], in0=ps, in1=st_t[:, 0, c0:c0 + n],
                                                   op=ALU.mult))
                    S.op('act', [r_sgt(buf)], [r_sgt(buf)],
                         lambda e: e.activation(out=sgt_t[:, buf, mm, c0:c0 + n], in_=sgt_t[:, buf, mm, c0:c0 + n],
                                                func=AF.Silu))
                proj_fm_one(w_gate[l], fp, xn_act, pieces, evac_gate, before_evac=(emit_stats if fp == 0 else None))

                def evac_up(m, pi, c0, n, ps, rps, buf=buf, fp=fp):
                    mm = m % 2
                    o, ro = U_bf(fp * 2 + mm, c0, n)
                    S.op('dve', [rps, r_sgt(buf)], [ro],
                         lambda e: e.tensor_tensor(out=o, in0=ps, in1=sgt_t[:, buf, mm, c0:c0 + n], op=ALU.mult))
                proj_fm_one(w_up[l], fp, xn_act, pieces, evac_up)
            proj_fm(w_down[l], FC, 4, 0, KC // 2, lambda k, c0, n: U_bf(k, c0, n), pieces,
                    make_evac_post(V_FFN_POST + l, 'xn'))
            postnorm_residual('xn', cols, pieces, final=final, epsv_row=3)

        def proj_fm_one(W2d, mp, act_fn, pieces, evac_fn, before_evac=None):
            banks = take_half(2 * len(pieces))
            slab, sreg, si = next_slab(W2d, 0, KC, mp * 256, 256)
            if S.dry:
                return
            if before_evac is not None:
                for k in range(KC):
                    def emit_k(e, k=k):
                        inst = None
                        for m in range(2):
                            for pi, (c0, n) in enumerate(pieces):
                                b = banks[m * len(pieces) + pi]
                                inst = e.matmul(psb[b][:, 0:n], lhsT=slab[:, k, m * 128:(m + 1) * 128],
                                                rhs=act_fn(k, c0, n)[0], start=(k == 0), stop=(k == KC - 1))
                        return inst
                    S.op('pe', [sreg] + [act_fn(k, c0, n)[1] for (c0, n) in pieces], [r_ps(b) for b in banks], emit_k)
            else:
                reads = [sreg] + [act_fn(k, c0, n)[1] for k in range(KC) for (c0, n) in pieces]

                def emit(e):
                    inst = None
                    for m in range(2):
                        for k in range(KC):
                            for pi, (c0, n) in enumerate(pieces):
                                b = banks[m * len(pieces) + pi]
                                inst = e.matmul(psb[b][:, 0:n], lhsT=slab[:, k, m * 128:(m + 1) * 128],
                                                rhs=act_fn(k, c0, n)[0], start=(k == 0), stop=(k == KC - 1))
                    return inst
                S.op('pe', reads, [r_ps(b) for b in banks], emit)
            after_consume(si)
            if before_evac is not None:
                before_evac()
            for m in range(2):
                for pi, (c0, n) in enumerate(pieces):
                    b = banks[m * len(pieces) + pi]
                    evac_fn(mp * 2 + m, pi, c0, n, psb[b][:, 0:n], r_ps(b))

        FULL = (0, TC)
        PIECES_FULL = [(0, 280), (280, 280)]
        COREC = (HL, CORE)
        PIECES_CORE = [(HL, CORE)]

        def mixer_pool():
            cols, pieces = FULL, PIECES_FULL
            emit_stats = prenorm_deferred(V_MIX_PRE + 0, cols, pieces, 'U', want_maskr=True)
            RC_BASE = 32 * 1120
            for gi, w in enumerate(POOL_WINDOWS):
                sw, sr = window_sum(maskp_t[:, :], R_MASK, w)
                o, ro = U_f32(gi, 0, TC, base=RC_BASE)
                S.op('dve', [sr], [ro], lambda e, sw=sw, o=o: e.tensor_scalar(out=o, in0=sw, scalar1=1.0, scalar2=None, op0=ALU.max))
                S.op('dve', [ro], [ro], lambda e, o=o: e.reciprocal(out=o, in_=o))

            def evac_in(m, pi, c0, n, ps, rps):
                S.op('dve', [rps, r_st(2)], [r_ht(m, PADW + c0, n)],
                     lambda e: e.tensor_tensor(out=ht_t[:, m, PADW + c0:PADW + c0 + n], in0=ps,
                                               in1=st_t[:, 2, c0:c0 + n], op=ALU.mult))
            proj_fm(pool_w_in, KC, 1, 0, KC // 2, xn_act, pieces, evac_in, before_evac=emit_stats)
            for k in range(KC):
                gi = k // 4
                w = POOL_WINDOWS[gi]
                sw, sr = window_sum(ht_t[:, k, :], r_ht(k, 0, HTW), w)
                rc, rcr = U_f32(gi, 0, TC, base=RC_BASE)
                S.op('dve', [sr, rcr], [r_scr(2, 0, TC)],
                     lambda e, sw=sw, rc=rc: e.tensor_tensor(out=scr_t[:, 2, 0:TC], in0=sw, in1=rc, op=ALU.mult))
                o, ro = U_bf(k, 0, TC)
                S.op('dve', [r_scr(2, 0, TC), r_ht(k, PADW, TC)], [ro],
                     lambda e, o=o, k=k: e.tensor_tensor(out=o, in0=scr_t[:, 2, 0:TC], in1=ht_t[:, k, PADW:PADW + TC],
                                                         op=ALU.subtract))
            for g in range(4):
                for mp in range(2):
                    banks = take_half(4)
                    slab, sreg, si = next_slab(pool_w_grp[g], 0, 4, mp * 256, 256)
                    if S.dry:
                        continue
                    reads = [sreg] + [U_bf(g * 4 + kk, c0, n)[1] for kk in range(4) for (c0, n) in pieces]

                    def emit(e, slab=slab, banks=banks, g=g):
                        inst = None
                        for m in range(2):
                            for kk in range(4):
                                for pi, (c0, n) in enumerate(pieces):
                                    inst = e.matmul(psb[banks[m * 2 + pi]][:, 0:n], lhsT=slab[:, kk, m * 128:(m + 1) * 128],
                                                    rhs=U_bf(g * 4 + kk, c0, n)[0], start=(kk == 0), stop=(kk == 3))
                        return inst
                    S.op('pe', reads, [r_ps(b) for b in banks], emit)
                    after_consume(si)
                    for m in range(2):
                        ch = g * 4 + mp * 2 + m
                        for pi, (c0, n) in enumerate(pieces):
                            b = banks[m * 2 + pi]
                            o, ro = U_bf(16 + ch, c0, n)
                            S.op('act', [r_ps(b), R_VECS], [ro],
                                 lambda e, o=o, b=b, n=n, ch=ch: e.mul(out=o, in_=psb[b][:, 0:n], mul=vec(V_POOL_SCALE, ch)))
            proj_fm(pool_w_out, KC, 1, 0, KC // 2, lambda k, c0, n: U_bf(16 + k, c0, n), pieces,
                    make_evac_post(V_MIX_POST + 0, 'xn'))
            postnorm_residual('xn', cols, pieces)

        def mixer_sc():
            cols, pieces = FULL, PIECES_FULL
            emit_stats = prenorm_deferred(V_MIX_PRE + 1, cols, pieces, 'U2', want_epsv=True, want_maskr=2)
            for jp in range(KC // 2):
                buf = jp % 2

                def evac_gc(m, pi, c0, n, ps, rps, buf=buf):
                    mm = m % 2
                    S.op('act', [rps], [r_sgt(buf)],
                         lambda e: e.copy(out=sgt_t[:, buf, mm, c0:c0 + n], in_=ps))
                proj_fm_one(sc_w_in[:, D:2 * D], jp, xn_act, pieces, evac_gc,
                            before_evac=(emit_stats if jp == 0 else None))

                def evac_h(m, pi, c0, n, ps, rps, buf=buf):
                    mm = m % 2
                    S.op('dve', [rps, r_sgt(buf)], [r_scr(mm, 1 + c0, n)],
                         lambda e: e.tensor_tensor(out=scr_t[:, mm, 1 + c0:1 + c0 + n], in0=ps,
                                                   in1=sgt_t[:, buf, mm, c0:c0 + n], op=ALU.mult))
                proj_fm_one(sc_w_in[:, 2 * D:3 * D], jp, xn_act, pieces, evac_h)
                if not S.dry:
                    for mm in range(2):
                        ch = jp * 2 + mm
                        S.op('dve', [], [r_scr(mm, 0, 1)], lambda e, mm=mm: e.memset(scr_t[:, mm, 0:1], 0.0))
                        S.op('dve', [], [r_scr(mm, TC + 1, 1)], lambda e, mm=mm: e.memset(scr_t[:, mm, TC + 1:TC + 2], 0.0))
                        S.op('dve', [r_scr(mm, 1, TC), r_st(2)], [r_scr(mm, 1, TC)],
                             lambda e, mm=mm: e.tensor_tensor(out=scr_t[:, mm, 1:1 + TC], in0=scr_t[:, mm, 1:1 + TC],
                                                              in1=st_t[:, 2, 0:TC], op=ALU.mult))
                        S.op('dve', [r_scr(mm), R_VECS], [r_scr(2 + mm, 0, TC)],
                             lambda e, mm=mm, ch=ch: e.tensor_scalar(out=scr_t[:, 2 + mm, 0:TC], in0=scr_t[:, mm, 1:1 + TC],
                                                                     scalar1=vec(V_SC_CONV + 1, ch), scalar2=None, op0=ALU.mult))
                        for j in (0, 2):
                            S.op('dve', [r_scr(mm), r_scr(2 + mm, 0, TC), R_VECS], [r_scr(2 + mm, 0, TC)],
                                 lambda e, mm=mm, ch=ch, j=j: e.scalar_tensor_tensor(
                                     out=scr_t[:, 2 + mm, 0:TC], in0=scr_t[:, mm, j:j + TC], scalar=vec(V_SC_CONV + j, ch),
                                     in1=scr_t[:, 2 + mm, 0:TC], op0=ALU.mult, op1=ALU.add))

                def evac_gb(m, pi, c0, n, ps, rps):
                    mm = m % 2
                    o, ro = U_bf(m, c0, n)
                    S.op('dve', [rps, r_scr(2 + mm, c0, n)], [ro],
                         lambda e: e.tensor_tensor(out=o, in0=ps, in1=scr_t[:, 2 + mm, c0:c0 + n], op=ALU.mult))
                proj_fm_one(sc_w_in[:, 0:D], jp, xn_act, pieces, evac_gb)
            proj_fm(sc_w_out, KC, 1, 0, KC // 2, lambda k, c0, n: U_bf(k, c0, n), pieces,
                    make_evac_post(V_MIX_POST + 1, 'xn'))
            postnorm_residual('xn', cols, pieces, epsv_row=3)

        def mixer_cf():
            cols, pieces = FULL, PIECES_FULL
            emit_stats = prenorm_deferred(V_MIX_PRE + 2, cols, pieces, 'U2', want_maskr=True)
            CIW = 592
            HB0 = 2816
            SQ0 = HB0 + KC * CORE

            def ci_bf(par, mm, c0, n):
                off = (par * 2 + mm) * CIW + c0
                return U_t[:, off:off + n], r_U(off * 2, (off + n) * 2)

            def hb_v(k):
                off = HB0 + k * CORE
                return U_t[:, off:off + CORE], r_U(off * 2, (off + CORE) * 2)

            def sq_v(k):
                off = SQ0 + k * CORE
                return U_t[:, off:off + CORE], r_U(off * 2, (off + CORE) * 2)
            e0 = PADW + HL

            def conv_pe(jp):
                par = jp % 2
                banks = take_half(2)
                for mm in range(2):
                    ch = jp * 2 + mm
                    slab, sreg, si = next_slab(dgw[ch], 0, 1, 0, 31 * 128)
                    if S.dry:
                        continue
                    b = banks[mm]

                    def emit(e, slab=slab, b=b, mm=mm):
                        inst = None
                        for j in range(31):
                            inst = e.matmul(psb[b][:, 0:CORE], lhsT=slab[:, 0, j * 128:(j + 1) * 128],
                                            rhs=ci_bf(par, mm, HL + j, CORE)[0], start=(j == 0), stop=(j == 30))
                        return inst
                    S.op('pe', [sreg, ci_bf(par, mm, 0, CIW)[1]], [r_ps(b)], emit)
                    after_consume(si)
                    S.op('act', [r_ps(b), R_VECS], [r_ht(ch, e0, CORE)],
                         lambda e, b=b, ch=ch: e.activation(out=ht_t[:, ch, e0:e0 + CORE], in_=psb[b][:, 0:CORE],
                                                            func=AF.Identity, bias=vec(V_CF_DWB, ch)))
                    o1, r1 = hb_v(ch)
                    S.op('act', [r_ps(b), R_VECS], [r1],
                         lambda e, b=b, ch=ch, o1=o1: e.activation(out=o1, in_=psb[b][:, 0:CORE], func=AF.Identity,
                                                                   bias=vec(V_CF_DWB, ch)))
                    o2, r2 = sq_v(ch)
                    S.op('act', [r_ps(b), R_VECS], [r2],
                         lambda e, b=b, ch=ch, o2=o2: e.activation(out=o2, in_=psb[b][:, 0:CORE], func=AF.Square,
                                                                   bias=vec(V_CF_DWB, ch)))

            for jp in range(KC // 2):
                buf = jp % 2
                par = jp % 2

                def evac_gate(m, pi, c0, n, ps, rps, buf=buf):
                    mm = m % 2
                    S.op('dve', [rps, r_st(0)], [r_sgt(buf)],
                         lambda e: e.tensor_tensor(out=sgt_t[:, buf, mm, c0:c0 + n], in0=ps, in1=st_t[:, 0, c0:c0 + n],
                                                   op=ALU.mult))
                    S.op('act', [r_sgt(buf)], [r_sgt(buf)],
                         lambda e: e.activation(out=sgt_t[:, buf, mm, c0:c0 + n], in_=sgt_t[:, buf, mm, c0:c0 + n],
                                                func=AF.Sigmoid))
                proj_fm_one(cf_w_in[:, D:2 * D], jp, xn_act, pieces, evac_gate,
                            before_evac=(emit_stats if jp == 0 else None))

                def evac_a(m, pi, c0, n, ps, rps, buf=buf, par=par):
                    mm = m % 2
                    o, ro = ci_bf(par, mm, 15 + c0, n)
                    S.op('dve', [rps, r_sgt(buf)], [ro],
                         lambda e: e.tensor_tensor(out=o, in0=ps, in1=sgt_t[:, buf, mm, c0:c0 + n], op=ALU.mult))
                proj_fm_one(cf_w_in[:, 0:D], jp, xn_act, pieces, evac_a)
                if not S.dry:
                    for mm in range(2):
                        cm, cmr = ci_bf(par, mm, 15, TC)
                        S.op('dve', [cmr, r_st(2)], [cmr],
                             lambda e, cm=cm: e.tensor_tensor(out=cm, in0=cm, in1=st_t[:, 2, 0:TC], op=ALU.mult))
                if jp >= 1:
                    conv_pe(jp - 1)
            conv_pe(KC // 2 - 1)
            if S.dry:
                proj_fm(cf_w_out, KC, 1, 0, KC // 2, None, PIECES_CORE, None)
                return
            cols, pieces = COREC, PIECES_CORE
            bks = take_half(2)

            def emit_s(e):
                inst = None
                for which, fn in enumerate((hb_v, sq_v)):
                    for k in range(KC):
                        inst = e.matmul(psb[bks[which]][:, 0:CORE], lhsT=ones_b[:, :], rhs=fn(k)[0],
                                        start=(k == 0), stop=(k == KC - 1))
                return inst
            S.op('pe', [R_CONST, r_U(HB0 * 2, (SQ0 + KC * CORE) * 2)], [r_ps(b) for b in bks], emit_s)
            sl = slice(HL, HL + CORE)
            S.op('act', [r_ps(bks[0])], [r_st(2)],
                 lambda e: e.mul(out=st_t[:, 2, sl], in_=psb[bks[0]][:, 0:CORE], mul=1.0 / D))
            S.op('act', [r_ps(bks[1])], [r_st(1)],
                 lambda e: e.mul(out=st_t[:, 1, sl], in_=psb[bks[1]][:, 0:CORE], mul=1.0 / D))
            S.op('dve', [r_st(2)], [r_st(0)],
                 lambda e: e.tensor_tensor(out=st_t[:, 0, sl], in0=st_t[:, 2, sl], in1=st_t[:, 2, sl], op=ALU.mult))
            S.op('dve', [r_st(1), r_st(0)], [r_st(1)],
                 lambda e: e.tensor_tensor(out=st_t[:, 1, sl], in0=st_t[:, 1, sl], in1=st_t[:, 0, sl], op=ALU.subtract))
            S.op('dve', [r_st(1)], [r_st(1)],
                 lambda e: e.tensor_scalar(out=st_t[:, 1, sl], in0=st_t[:, 1, sl], scalar1=0.0, scalar2=EPS,
                                           op0=ALU.max, op1=ALU.add))
            S.op('act', [r_st(1)], [r_st(1)], lambda e: e.activation(out=st_t[:, 1, sl], in_=st_t[:, 1, sl], func=AF.Sqrt))
            S.op('dve', [r_st(1)], [r_st(1)], lambda e: e.reciprocal(out=st_t[:, 1, sl], in_=st_t[:, 1, sl]))
            for k in range(KC):
                S.op('dve', [r_ht(k, e0, CORE), r_st(2)], [r_ht(k, e0, CORE)],
                     lambda e, k=k: e.tensor_tensor(out=ht_t[:, k, e0:e0 + CORE], in0=ht_t[:, k, e0:e0 + CORE],
                                                    in1=st_t[:, 2, sl], op=ALU.subtract))
                S.op('dve', [r_ht(k, e0, CORE), r_st(1)], [r_ht(k, e0, CORE)],
                     lambda e, k=k: e.tensor_tensor(out=ht_t[:, k, e0:e0 + CORE], in0=ht_t[:, k, e0:e0 + CORE],
                                                    in1=st_t[:, 1, sl], op=ALU.mult))
                o, ro = U_bf(k, HL, CORE)
                S.op('act', [r_ht(k, e0, CORE), R_VECS], [ro],
                     lambda e, k=k, o=o: e.activation(out=o, in_=ht_t[:, k, e0:e0 + CORE], func=AF.Silu,
                                                      bias=vec(V_CF_LNB, k), scale=vec(V_CF_LNG, k)))
            proj_fm(cf_w_out, KC, 1, 0, KC // 2, lambda k, c0, n: U_bf(k, c0, n), pieces,
                    make_evac_post(V_MIX_POST + 2, 'xn'))
            postnorm_residual('xn', cols, pieces)

        def mixer_sg():
            cols, pieces = COREC, PIECES_CORE
            emit_stats_fm = prenorm_deferred(V_MIX_PRE + 3, cols, pieces, 'U2')

            def emit_stats():
                emit_stats_fm()
                bk = take_half(1)
                state['half'] ^= 1
                reads = [R_CONST] + [sq_view('U2', k, HL, CORE)[1] for k in range(KC)]

                def emit_c(e):
                    inst = None
                    for tc_ in range(4):
                        for k in range(KC):
                            inst = e.matmul(psb[bk[0]][:, tc_:tc_ + 1], lhsT=sq_view('U2', k, HL + tc_ * 128, 128)[0],
                                            rhs=ones_b[:, 0:1], start=(k == 0), stop=(k == KC - 1))
                    return inst
                S.op('pe', reads, [r_ps(bk[0])], emit_c)
                S.op('act', [r_ps(bk[0]), r_tiny(0)], [r_tiny(40, 4)],
                     lambda e: e.activation(out=tiny_t[:, 40:44], in_=psb[bk[0]][:, 0:4], func=AF.Sqrt,
                                            bias=tiny_t[:, 0:1], scale=1.0 / D))
                S.op('dve', [r_tiny(40, 4)], [r_tiny(40, 4)],
                     lambda e: e.reciprocal(out=tiny_t[:, 40:44], in_=tiny_t[:, 40:44]))

            def evac_u(m, pi, c0, n, ps, rps):
                S.op('dve', [rps, r_st(0)], [r_ht(m, PADW + c0, n)],
                     lambda e: e.tensor_tensor(out=ht_t[:, m, PADW + c0:PADW + c0 + n], in0=ps, in1=st_t[:, 0, c0:c0 + n],
                                               op=ALU.mult))
                S.op('act', [r_ht(m, PADW + c0, n)], [r_ht(m, PADW + c0, n)],
                     lambda e: e.activation(out=ht_t[:, m, PADW + c0:PADW + c0 + n], in_=ht_t[:, m, PADW + c0:PADW + c0 + n],
                                            func=AF.Gelu))

            def vtok(tc_, f0, n):
                b0 = (tc_ * D + f0) * 4
                return U_t[:, b0 // 2: b0 // 2 + 2 * n].bitcast(F32), r_U(b0, b0 + 4 * n)

            def vln(tc_, f0, n):
                b0 = 32768 + (tc_ * D + f0) * 2
                return U_t[:, b0 // 2: b0 // 2 + n], r_U(b0, b0 + 2 * n)
            for jp in range(KC // 2):
                banks = take_half(4)
                slab, sreg, si = next_slab(sg_w_in, 0, KC, D + jp * 256, 256)
                if S.dry:
                    continue
                if jp == 0:
                    for k in range(KC):
                        def emit_k(e, slab=slab, banks=banks, k=k):
                            inst = None
                            for tc_ in range(4):
                                c = k * TC + HL + tc_ * 128
                                inst = e.matmul(psb[banks[tc_]][:, 0:256], lhsT=xn_t[:, c:c + 128], rhs=slab[:, k, 0:256],
                                                start=(k == 0), stop=(k == KC - 1))
                            return inst
                        S.op('pe', [sreg, r_xn(k, HL, CORE)], [r_ps(b) for b in banks], emit_k)
                else:
                    reads = [sreg] + [r_xn(k, HL, CORE) for k in range(KC)]

                    def emit(e, slab=slab, banks=banks):
                        inst = None
                        for tc_ in range(4):
                            for k in range(KC):
                                c = k * TC + HL + tc_ * 128
                                inst = e.matmul(psb[banks[tc_]][:, 0:256], lhsT=xn_t[:, c:c + 128], rhs=slab[:, k, 0:256],
                                                start=(k == 0), stop=(k == KC - 1))
                        return inst
                    S.op('pe', reads, [r_ps(b) for b in banks], emit)
                after_consume(si)
                if jp == 0:
                    emit_stats()
                for tc_ in range(4):
                    o, ro = vtok(tc_, jp * 256, 256)
                    S.op('act', [r_ps(banks[tc_]), r_tiny(40 + tc_)], [ro],
                         lambda e, o=o, b=banks[tc_], tc_=tc_: e.activation(out=o, in_=psb[b][:, 0:256], func=AF.Gelu,
                                                                            scale=tiny_t[:, 40 + tc_:41 + tc_]))
            proj_fm(sg_w_in, KC, 1, 0, KC // 2, xn_act, pieces, evac_u)
            if not S.dry:
                for tc_ in range(4):
                    v_ap, v_r = vtok(tc_, 0, D)
                    j_ap, j_r = vln(tc_, 0, D)
                    c = 8 + tc_ * 8
                    S.op('dve', [v_r], [r_tiny(c)], lambda e, v_ap=v_ap, c=c: e.reduce_sum(out=tiny_t[:, c:c + 1], in_=v_ap, axis=AX.X))
                    S.op('dve', [], [r_tiny(c + 1)], lambda e, c=c: e.memset(tiny_t[:, c + 1:c + 2], 0.0))
                    S.op('act', [v_r, r_tiny(c + 1)], [j_r, r_tiny(c + 1)],
                         lambda e, v_ap=v_ap, j_ap=j_ap, c=c: e.activation(out=j_ap, in_=v_ap, func=AF.Square,
                                                                           accum_out=tiny_t[:, c + 1:c + 2]))
                    S.op('dve', [r_tiny(c, 2)], [r_tiny(c, 2)],
                         lambda e, c=c: e.tensor_scalar(out=tiny_t[:, c:c + 2], in0=tiny_t[:, c:c + 2], scalar1=1.0 / D,
                                                        scalar2=None, op0=ALU.mult))
                    S.op('dve', [r_tiny(c)], [r_tiny(c + 2)],
                         lambda e, c=c: e.tensor_tensor(out=tiny_t[:, c + 2:c + 3], in0=tiny_t[:, c:c + 1],
                                                        in1=tiny_t[:, c:c + 1], op=ALU.mult))
                    S.op('dve', [r_tiny(c + 1, 2)], [r_tiny(c + 1)],
                         lambda e, c=c: e.tensor_tensor(out=tiny_t[:, c + 1:c + 2], in0=tiny_t[:, c + 1:c + 2],
                                                        in1=tiny_t[:, c + 2:c + 3], op=ALU.subtract))
                    S.op('dve', [r_tiny(c + 1)], [r_tiny(c + 1)],
                         lambda e, c=c: e.tensor_scalar(out=tiny_t[:, c + 1:c + 2], in0=tiny_t[:, c + 1:c + 2], scalar1=0.0,
                                                        scalar2=EPS, op0=ALU.max, op1=ALU.add))
                    S.op('act', [r_tiny(c + 1)], [r_tiny(c + 1)],
                         lambda e, c=c: e.activation(out=tiny_t[:, c + 1:c + 2], in_=tiny_t[:, c + 1:c + 2], func=AF.Sqrt))
                    S.op('dve', [r_tiny(c + 1)], [r_tiny(c + 1)],
                         lambda e, c=c: e.reciprocal(out=tiny_t[:, c + 1:c + 2], in_=tiny_t[:, c + 1:c + 2]))
                    S.op('dve', [v_r, r_tiny(c, 2)], [j_r],
                         lambda e, v_ap=v_ap, j_ap=j_ap, c=c: e.tensor_scalar(out=j_ap, in0=v_ap, scalar1=tiny_t[:, c:c + 1],
                                                                              scalar2=tiny_t[:, c + 1:c + 2],
                                                                              op0=ALU.subtract, op1=ALU.mult))
                for kq in range(4):
                    banks = take_half(4)
                    reads = [R_CONST] + [vln(tc_, kq * 512, 512)[1] for tc_ in range(4)]

                    def emit(e, kq=kq, banks=banks):
                        inst = None
                        for kk in range(4):
                            k = kq * 4 + kk
                            for tc_ in range(4):
                                inst = e.matmul(psb[banks[kk]][:, tc_ * 128:(tc_ + 1) * 128],
                                                lhsT=vln(tc_, k * 128, 128)[0], rhs=wsT_t[:, k // 2, :],
                                                start=True, stop=True)
                        return inst
                    S.op('pe', reads, [r_ps(b) for b in banks], emit)
                    for kk in range(4):
                        k = kq * 4 + kk
                        h = k // 2
                        b = banks[kk]
                        S.op('dve', [R_R, R_CONST, R_VECS], [r_scr(0, 0, 128)],
                             lambda e, k=k, h=h: e.scalar_tensor_tensor(out=scr_t[:, 0, 0:128], in0=R_t[:, h, :],
                                                                        scalar=vec(V_SG_LNB, k), in1=bsb_t[:, h, :],
                                                                        op0=ALU.mult, op1=ALU.add))
                        for tc_ in range(4):
                            S.op('dve', [r_ps(b), r_scr(0, 0, 128), R_VECS], [r_scr(1, tc_ * 128, 128)],
                                 lambda e, k=k, b=b, tc_=tc_: e.scalar_tensor_tensor(
                                     out=scr_t[:, 1, tc_ * 128:(tc_ + 1) * 128], in0=psb[b][:, tc_ * 128:(tc_ + 1) * 128],
                                     scalar=vec(V_SG_LNG, k), in1=scr_t[:, 0, 0:128], op0=ALU.mult, op1=ALU.add))
                        S.op('dve', [r_scr(1, 0, 512), r_ht(k, PADW + HL, CORE)], [r_xn(k, HL, CORE)],
                             lambda e, k=k: e.tensor_tensor(out=xn_t[:, k * TC + HL:k * TC + HL + CORE], in0=scr_t[:, 1, 0:512],
                                                            in1=ht_t[:, k, PADW + HL:PADW + HL + CORE], op=ALU.mult))
            proj_fm(sg_w_out, KC, 1, 0, KC // 2, xn_act, pieces, make_evac_post(V_MIX_POST + 3, 'U'))
            postnorm_residual('U', cols, pieces)

        def load_x(i):
            for k in range(KC):
                S.dma('sp', f'xl{k}', [], [r_x(k, 0, TC)],
                      lambda e, k=k: e.dma_start(out=x_t[:, k, :], in_=xin[i][k * 128:(k + 1) * 128, :]))

        def load_mask(i):
            S.dma('sp', 'mld', [], [R_MASK],
                  lambda e: e.dma_start(out=maskp_t[:, PADW:PADW + TC], in_=maskd[i]))

        def tile_body(i):
            if i == 0 and not S.dry:
                load_x(0)
                load_mask(0)
            mixer_pool()
            ffn(0, FULL, PIECES_FULL)
            mixer_sc()
            ffn(1, FULL, PIECES_FULL)
            mixer_cf()
            if i + 1 < NT and not S.dry:
                load_mask(i + 1)
            ffn(2, COREC, PIECES_CORE)
            mixer_sg()
            ffn(3, COREC, PIECES_CORE, final=True)
            if not S.dry:
                if i + 1 < NT:
                    load_x(i + 1)
                S.dma('sp', 'yst', [('ht', 0, KC * HTW * 4)], [],
                      lambda e: e.dma_start(out=yout[i].rearrange("(k p) t -> p k t", p=128),
                                            in_=ht_t[:, :, PADW + HL:PADW + HL + CORE]))

        def prologue():
            S.dma('sp', 'cst', [], [R_VECS], lambda e: e.dma_start(out=vecs_t[:, :, :].rearrange("p v k -> p (v k)"), in_=vecsd))
            S.dma('sp', 'cst', [], [R_CONST], lambda e: e.dma_start(out=bsb_t[:, :, :].rearrange("p h q -> p (h q)"), in_=bsbd))
            S.dma('pool', 'cst', [], [R_CONST], lambda e: e.dma_start(out=wsT_t[:, :, :].rearrange("p h q -> p (h q)"), in_=wsTd))
            S.dma('pool', 'cst', [], [R_CONST], lambda e: e.dma_start(out=ident_b[:, :], in_=identd))
            S.op('dve', [], [R_CONST], lambda e: e.memset(ones_f[:, :], 1.0))
            S.op('dve', [], [R_CONST], lambda e: e.memset(ones_b[:, :], 1.0))
            S.op('dve', [], [r_tiny(0)], lambda e: e.memset(tiny_t[:, 0:1], EPS))
            S.op('dve', [], [R_MASK], lambda e: e.memset(maskp_t[:, :], 0.0))
            S.op('dve', [], [('ht', 0, KC * HTW * 4)], lambda e: e.memset(ht_t[:, :, :].rearrange("p k t -> p (k t)"), 0.0))
            S.op('dve', [], [r_scr(0), r_scr(1), r_scr(2), r_scr(3)],
                 lambda e: e.memset(scr_t[:, :, :].rearrange("p k t -> p (k t)"), 0.0))
            for hb in range(2):
                banks = take_half(1)

                def emit(e, hb=hb, banks=banks):
                    inst = None
                    for hh in range(4):
                        inst = e.matmul(psb[banks[0]][:, hh * 128:(hh + 1) * 128], lhsT=ones_b[:, :],
                                        rhs=wsT_t[:, hb * 4 + hh, :], start=True, stop=True)
                    return inst
                S.op('pe', [R_CONST], [r_ps(banks[0])], emit)
                S.op('act', [r_ps(banks[0])], [R_R],
                     lambda e, hb=hb, banks=banks: e.activation(out=R_t[:, hb * 4:(hb + 1) * 4, :].rearrange("p h q -> p (h q)"),
                                                                in_=psb[banks[0]][:, 0:512], func=AF.Identity))

        S.dry = True
        for i in range(NT):
            tile_body(i)
        S.dry = False
        state['half'] = 0
        prologue()
        for i in range(NT):
            tile_body(i)
        assert state['next_consume'] == len(plan), (state['next_consume'], len(plan))
        S.final_wait('sp', ['yst'])

        with nc.Block() as block:
            @block.tensor
            def _(e):
                for f in S.prog['pe']:
                    f(e)

            @block.scalar
            def _(e):
                for f in S.prog['act']:
                    f(e)

            @block.vector
            def _(e):
                for f in S.prog['dve']:
                    f(e)

            @block.gpsimd
            def _(e):
                for f in S.prog['pool']:
                    f(e)

            @block.sync
            def _(e):
                for f in S.prog['sp']:
                    f(e)
        print("program sizes", {k: len(v) for k, v in S.prog.items()}, "slabs", len(plan), flush=True)
    return nc


def _tiles_layout(seqs, n_cores, NT):
    tiles = []
    for si, a in enumerate(seqs):
        assert a.shape[0] % CORE == 0
        for s0 in range(0, a.shape[0], CORE):
            tiles.append((si, s0))
    assert len(tiles) == n_cores * NT, (len(tiles), n_cores, NT)
    return tiles


def run_trunk_device(seqs, W, n_cores, NT):
    tiles = _tiles_layout(seqs, n_cores, NT)
    xin = np.zeros((n_cores, NT, D, TC), np.float32)
    maskb = np.zeros((n_cores, NT, 128, TC), np.float32)
    for t, (si, s0) in enumerate(tiles):
        c, i = divmod(t, NT)
        a = seqs[si]
        Sl = a.shape[0]
        lo, hi = max(0, s0 - HL), min(Sl, s0 + CORE + HL)
        o0 = lo - (s0 - HL)
        xin[c, i, :, o0:o0 + (hi - lo)] = a[lo:hi, :].T
        maskb[c, i, :, o0:o0 + (hi - lo)] = 1.0

    def pk(v):
        return np.asarray(v, np.float32).reshape(KC, 128).T
    vecs = np.zeros((128, NV, KC), np.float32)
    for l in range(4):
        vecs[:, V_MIX_PRE + l] = pk(W["mix_pre_g"][l])
        vecs[:, V_MIX_POST + l] = pk(W["mix_post_g"][l])
        vecs[:, V_FFN_PRE + l] = pk(W["ffn_pre_g"][l])
        vecs[:, V_FFN_POST + l] = pk(W["ffn_post_g"][l])
    vecs[:, V_POOL_SCALE] = pk(W["pool_scale"][0])
    vecs[:, V_CF_DWB] = pk(W["cf_dw_b"][0])
    vecs[:, V_CF_LNG] = pk(W["cf_ln_g"][0])
    vecs[:, V_CF_LNB] = pk(W["cf_ln_b"][0])
    vecs[:, V_SG_LNG] = pk(W["sg_ln_g"][0])
    vecs[:, V_SG_LNB] = pk(W["sg_ln_b"][0])
    for j in range(3):
        vecs[:, V_SC_CONV + j] = pk(W["sc_conv_w"][0, j])
    for j in range(31):
        vecs[:, V_CF_DW + j] = pk(W["cf_dw_w"][0, j])
    vecs = np.ascontiguousarray(vecs.reshape(128, NV * KC))
    dgw = np.zeros((KC, 128, 31, 128), np.float32)
    cw = np.asarray(W["cf_dw_w"][0], np.float32).reshape(31, KC, 128)
    pidx = np.arange(128)
    for j in range(31):
        dgw[:, pidx, j, pidx] = cw[j]
    dgw = dgw.reshape(KC, 128, 31 * 128)
    wsT = np.ascontiguousarray(np.asarray(W["sg_w_s"][0], np.float32).transpose(2, 0, 1).reshape(128, 8 * 128))
    bsb = np.ascontiguousarray(np.broadcast_to(np.asarray(W["sg_b_s"][0], np.float32).reshape(1, 8 * 128), (128, 8 * 128)))

    shared = {
        "vecs": vecs, "wsT": wsT, "bsb": bsb, "ident": np.eye(128, dtype=np.float32), "dgw": dgw,
        "ffn_w_gate": np.asarray(W["ffn_w_gate"], np.float32),
        "ffn_w_up": np.asarray(W["ffn_w_up"], np.float32),
        "ffn_w_down": np.asarray(W["ffn_w_down"], np.float32),
        "pool_w_in": np.asarray(W["pool_w_in"][0], np.float32),
        "pool_w_grp": np.asarray(W["pool_w_grp"][0], np.float32),
        "pool_w_out": np.asarray(W["pool_w_out"][0], np.float32),
        "sc_w_in": np.asarray(W["sc_w_in"][0], np.float32),
        "sc_w_out": np.asarray(W["sc_w_out"][0], np.float32),
        "cf_w_in": np.asarray(W["cf_w_in"][0], np.float32),
        "cf_w_out": np.asarray(W["cf_w_out"][0], np.float32),
        "sg_w_in": np.asarray(W["sg_w_in"][0], np.float32),
        "sg_w_out": np.asarray(W["sg_w_out"][0], np.float32),
    }
    nc = build_program(NT)
    in_maps = []
    for c in range(n_cores):
        m = dict(shared)
        m["xin"] = xin[c]
        m["maskb"] = maskb[c]
        in_maps.append(m)
    res = run_bass_kernel_spmd(nc, in_maps, core_ids=list(range(n_cores)))
    outs = [np.empty_like(a) for a in seqs]
    for t, (si, s0) in enumerate(tiles):
        c, i = divmod(t, NT)
        outs[si][s0:s0 + CORE, :] = res.results[c]["yout"][i].T
    return outs


def kernel(**inputs):
    xp = np.asarray(inputs["x_prompt"], np.float32)
    xs = np.asarray(inputs["x_sample"], np.float32)
    seqs = [xp[b] for b in range(xp.shape[0])] + [xs[b] for b in range(xs.shape[0])]
    outs = run_trunk_device(seqs, inputs, 8, 12)
    y_prompt = np.stack(outs[:xp.shape[0]], axis=0)
    y_sample = np.stack(outs[xp.shape[0]:], axis=0)
    return (y_prompt, y_sample)
```
